# Optimizing a Trainium2 kernel written in Bass

```python
import math
import jax
import jax.numpy as jnp
from jax import lax
import numpy as np

D_MODEL = 1024
BATCH = 8
SEQ = 4096
DEPTH = 4

CTX_LEN = 256
GRID_W = 64
HEAD_DIM = 64
ROPE_BASE = 10000.0
N_MOD = 9
FFN_DIM = 2816
A_HEADS = 4
A_VDIM = 2 * HEAD_DIM
B_HEADS = 4
B_KDIM = 64
B_VDIM = 128
GATE_RANK = 16
GATE_TAU = 16.0
GLA_CHUNK = 64
C_HEADS = 8
C_KV_HEADS = 2
WINDOW = 128
Q_BLOCK = 128
KEY_SPAN = Q_BLOCK + 2 * WINDOW
NEG_INF = -1e30
MIX_SIZES = (A_HEADS * 2 * HEAD_DIM, A_HEADS * 2 * HEAD_DIM, A_HEADS * A_VDIM,
             B_HEADS * B_KDIM, B_HEADS * B_KDIM, B_HEADS * B_VDIM, 2 * GATE_RANK, B_HEADS * B_VDIM,
             C_HEADS * HEAD_DIM, C_KV_HEADS * HEAD_DIM, C_KV_HEADS * HEAD_DIM, 3 * D_MODEL)
IN_COLS = sum(MIX_SIZES)
SPLITS = tuple(int(s) for s in np.cumsum(MIX_SIZES)[:-1])

kernel_name = 'hybrid_prefix_dit_block'


def _rmsnorm(x, g, eps=1e-6):
    xf = x.astype(jnp.float32)
    y = xf * lax.rsqrt(jnp.mean(xf * xf, axis=-1, keepdims=True) + eps)
    return (y * g.astype(jnp.float32)).astype(x.dtype)


def _modulate(h, shift, scale):
    return h * (1.0 + scale) + shift


def _swiglu(h, w_up, w_down):
    u, v = jnp.split(h @ w_up, 2, axis=-1)
    return (jax.nn.silu(u) * v) @ w_down


def _rope_tables(n_tok):
    rows = n_tok // GRID_W
    row = jnp.repeat(jnp.arange(rows, dtype=jnp.float32), GRID_W)
    col = jnp.tile(jnp.arange(GRID_W, dtype=jnp.float32), rows)
    n_freq = HEAD_DIM // 4
    freqs = jnp.power(ROPE_BASE, -jnp.arange(n_freq, dtype=jnp.float32) / n_freq)
    ar = row[:, None] * freqs
    ac = col[:, None] * freqs
    ang = jnp.concatenate([ar, ar, ac, ac], axis=-1)
    return jnp.cos(ang), jnp.sin(ang)


def _apply_rope(x, cos, sin):
    bshape = (x.shape[1],) + (1,) * (x.ndim - 3) + (HEAD_DIM,)
    cos = cos.reshape(bshape)
    sin = sin.reshape(bshape)
    xs = x.reshape(x.shape[:-1] + (2, 2, HEAD_DIM // 4))
    rot = jnp.concatenate([-xs[..., 1:, :], xs[..., :1, :]], axis=-2).reshape(x.shape)
    return (x * cos + rot * sin).astype(x.dtype)


def _diff_attend(q, k, v, lam):
    s = jnp.einsum('bqhmd,bkhmd->bhmqk', q, k).astype(jnp.float32) * (HEAD_DIM ** -0.5)
    p = jax.nn.softmax(s, axis=-1)
    a = p[:, :, 0] - lam * p[:, :, 1]
    return jnp.einsum('bhqk,bkhe->bqhe', a.astype(v.dtype), v)


def _diff_attention_latent(q, k_all, v_all, lam):
    bsz, n = q.shape[:2]
    nb = n // Q_BLOCK
    qb = jnp.moveaxis(q.reshape(bsz, nb, Q_BLOCK, A_HEADS, 2, HEAD_DIM), 1, 0)
    o = lax.map(lambda blk: _diff_attend(blk, k_all, v_all, lam), qb)
    return jnp.moveaxis(o, 0, 1).reshape(bsz, n, A_HEADS, A_VDIM)


def _gla_inputs(p, gate_w, gate_b):
    bsz, n = p[3].shape[:2]

    def heads(t, dh):
        return t.reshape(bsz, n, B_HEADS, dh).transpose(0, 2, 1, 3).astype(jnp.float32)

    q = heads(p[3], B_KDIM)
    k = heads(p[4], B_KDIM)
    v = heads(p[5], B_VDIM)
    log_a = []
    for d in range(2):
        z = p[6][..., d * GATE_RANK:(d + 1) * GATE_RANK] @ gate_w[d] + gate_b[d]
        log_a.append(heads(jax.nn.log_sigmoid(z.astype(jnp.float32)) / GATE_TAU, B_KDIM))
    return q, k, v, log_a


def _gla_states(k, v, log_a, s0):
    bsz, h, n, dk = k.shape
    nc = n // GLA_CHUNK
    kc = k.reshape(bsz, h, nc, GLA_CHUNK, dk)
    vc = v.reshape(bsz, h, nc, GLA_CHUNK, B_VDIM)
    cum = jnp.cumsum(log_a.reshape(bsz, h, nc, GLA_CHUNK, dk), axis=3)
    last = cum[:, :, :, -1]
    kv = jnp.einsum('bhnck,bhncv->bhnkv', kc * jnp.exp(last[:, :, :, None] - cum), vc)

    def step(state, inp):
        dec, upd = inp
        return dec[..., None] * state + upd, state

    final, starts = lax.scan(step, s0, (jnp.moveaxis(jnp.exp(last), 2, 0), jnp.moveaxis(kv, 2, 0)))
    return cum, jnp.moveaxis(starts, 0, 2), final


def _gla_outputs(q, k, v, cum, starts):
    bsz, h, n, dk = q.shape
    nc = n // GLA_CHUNK
    qe = q.reshape(bsz, h, nc, GLA_CHUNK, dk) * (dk ** -0.5) * jnp.exp(cum)
    ke = k.reshape(bsz, h, nc, GLA_CHUNK, dk) * jnp.exp(-cum)
    vc = v.reshape(bsz, h, nc, GLA_CHUNK, B_VDIM)
    idx = jnp.arange(GLA_CHUNK)
    earlier = idx[:, None] >= idx[None, :]
    att = jnp.where(earlier, jnp.einsum('bhnik,bhnjk->bhnij', qe, ke), 0.0)
    o = jnp.einsum('bhnij,bhnjv->bhniv', att, vc) + jnp.einsum('bhnik,bhnkv->bhniv', qe, starts)
    return o.reshape(bsz, h, n, B_VDIM)


def _gla_direction(lat, ctx_in, la_lat, la_ctx, reverse, with_ctx):
    q, k, v = lat
    qc, kc, vc = ctx_in
    if reverse:
        q, k, v, la_lat, qc, kc, vc, la_ctx = [jnp.flip(t, axis=2) for t in (q, k, v, la_lat, qc, kc, vc, la_ctx)]
    s0 = jnp.zeros(k.shape[:2] + (B_KDIM, B_VDIM), jnp.float32)
    cum_c, starts_c, final_c = _gla_states(kc, vc, la_ctx, s0)
    cum, starts, _ = _gla_states(k, v, la_lat, final_c)
    o = _gla_outputs(q, k, v, cum, starts)
    o_c = _gla_outputs(qc, kc, vc, cum_c, starts_c) if with_ctx else None
    if reverse:
        o = jnp.flip(o, axis=2)
        o_c = jnp.flip(o_c, axis=2) if with_ctx else None
    return o, o_c


def _window_attention_latent(q, k, v, k_ctx, v_ctx, sink):
    bsz, n = q.shape[:2]
    nb = n // Q_BLOCK
    g = C_HEADS // C_KV_HEADS
    n_ctx = k_ctx.shape[1]
    qb = jnp.moveaxis(q.reshape(bsz, nb, Q_BLOCK, C_KV_HEADS, g, HEAD_DIM), 1, 0)
    pad = ((0, 0), (WINDOW, WINDOW), (0, 0), (0, 0))
    kp = jnp.pad(k, pad)
    vp = jnp.pad(v, pad)
    sink_row = jnp.broadcast_to(sink.astype(jnp.float32).reshape(1, C_KV_HEADS, g, 1, 1),
                                (bsz, C_KV_HEADS, g, Q_BLOCK, 1))
    qi = jnp.arange(Q_BLOCK)[:, None]
    kj = jnp.arange(KEY_SPAN)[None, :]
    rel = kj - qi
    scale = HEAD_DIM ** -0.5

    def one_block(args):
        blk, qblk = args
        kb = lax.dynamic_slice_in_dim(kp, blk * Q_BLOCK, KEY_SPAN, axis=1)
        vb = lax.dynamic_slice_in_dim(vp, blk * Q_BLOCK, KEY_SPAN, axis=1)
        pos = blk * Q_BLOCK - WINDOW + kj
        valid = (rel >= 0) & (rel <= 2 * WINDOW) & (pos >= 0) & (pos < n)
        s_loc = jnp.einsum('bqhgd,bkhd->bhgqk', qblk, kb).astype(jnp.float32) * scale
        s_loc = jnp.where(valid, s_loc, NEG_INF)
        s_ctx = jnp.einsum('bqhgd,bkhd->bhgqk', qblk, k_ctx).astype(jnp.float32) * scale
        p = jax.nn.softmax(jnp.concatenate([s_loc, s_ctx, sink_row], axis=-1), axis=-1).astype(v.dtype)
        return (jnp.einsum('bhgqk,bkhd->bqhgd', p[..., :KEY_SPAN], vb)
                + jnp.einsum('bhgqk,bkhd->bqhgd', p[..., KEY_SPAN:KEY_SPAN + n_ctx], v_ctx))

    o = lax.map(one_block, (jnp.arange(nb), qb))
    return jnp.moveaxis(o, 0, 1).reshape(bsz, n, C_HEADS * HEAD_DIM)


def _window_attention_ctx(q, k, v, sink):
    bsz, n = q.shape[:2]
    g = C_HEADS // C_KV_HEADS
    qg = q.reshape(bsz, n, C_KV_HEADS, g, HEAD_DIM)
    s = jnp.einsum('bqhgd,bkhd->bhgqk', qg, k).astype(jnp.float32) * (HEAD_DIM ** -0.5)
    sink_row = jnp.broadcast_to(sink.astype(jnp.float32).reshape(1, C_KV_HEADS, g, 1, 1), (bsz, C_KV_HEADS, g, n, 1))
    p = jax.nn.softmax(jnp.concatenate([s, sink_row], axis=-1), axis=-1)[..., :n].astype(v.dtype)
    return jnp.einsum('bhgqk,bkhd->bqhgd', p, v).reshape(bsz, n, C_HEADS * HEAD_DIM)


def _token_mix(hx, hc, w_in, diff_lambda, diff_subln, gla_gate_w, gla_gate_b, gla_norm, swa_sink,
               w_br_a, w_br_b, w_br_c, w_out, lam_init, cos, sin, with_ctx):
    bsz, n, _ = hx.shape
    n_ctx = hc.shape[1]
    px = jnp.split(hx @ w_in, SPLITS, axis=-1)
    pc = jnp.split(hc @ w_in, SPLITS, axis=-1)

    aq = _apply_rope(px[0].reshape(bsz, n, A_HEADS, 2, HEAD_DIM), cos, sin)
    ak = _apply_rope(px[1].reshape(bsz, n, A_HEADS, 2, HEAD_DIM), cos, sin)
    av = px[2].reshape(bsz, n, A_HEADS, A_VDIM)
    caq = pc[0].reshape(bsz, n_ctx, A_HEADS, 2, HEAD_DIM)
    cak = pc[1].reshape(bsz, n_ctx, A_HEADS, 2, HEAD_DIM)
    cav = pc[2].reshape(bsz, n_ctx, A_HEADS, A_VDIM)
    lp = diff_lambda.astype(jnp.float32)
    lam = jnp.exp(jnp.sum(lp[0] * lp[1])) - jnp.exp(jnp.sum(lp[2] * lp[3])) + lam_init

    def diff_out(o):
        return (_rmsnorm(o, diff_subln) * (1.0 - lam_init)).reshape(o.shape[0], o.shape[1], A_HEADS * A_VDIM)

    o_a = diff_out(_diff_attention_latent(aq, jnp.concatenate([ak, cak], axis=1),
                                          jnp.concatenate([av, cav], axis=1), lam))

    q, k, v, la = _gla_inputs(px, gla_gate_w, gla_gate_b)
    qc, kc, vc, lac = _gla_inputs(pc, gla_gate_w, gla_gate_b)
    of, ofc = _gla_direction((q, k, v), (qc, kc, vc), la[0], lac[0], False, with_ctx)
    ob, obc = _gla_direction((q, k, v), (qc, kc, vc), la[1], lac[1], True, with_ctx)

    def gla_out(o, r):
        o = _rmsnorm(o.transpose(0, 2, 1, 3), gla_norm)
        return o.reshape(o.shape[0], o.shape[1], B_HEADS * B_VDIM).astype(r.dtype) * jax.nn.silu(r)

    o_b = gla_out(of + ob, px[7])

    cq = _apply_rope(px[8].reshape(bsz, n, C_HEADS, HEAD_DIM), cos, sin)
    ck = _apply_rope(px[9].reshape(bsz, n, C_KV_HEADS, HEAD_DIM), cos, sin)
    cv = px[10].reshape(bsz, n, C_KV_HEADS, HEAD_DIM)
    ccq = pc[8].reshape(bsz, n_ctx, C_HEADS, HEAD_DIM)
    cck = pc[9].reshape(bsz, n_ctx, C_KV_HEADS, HEAD_DIM)
    ccv = pc[10].reshape(bsz, n_ctx, C_KV_HEADS, HEAD_DIM)
    o_c = _window_attention_latent(cq, ck, cv, cck, ccv, swa_sink)

    def merge(gate_cols, oa, obr, oc):
        ga, gb, gc = jnp.split(jax.nn.sigmoid(gate_cols), 3, axis=-1)
        y = ga * (oa @ w_br_a) + gb * (obr @ w_br_b) + gc * (oc @ w_br_c)
        return y @ w_out

    out_x = merge(px[11], o_a, o_b, o_c)
    if not with_ctx:
        return out_x, None
    oa_c = diff_out(_diff_attend(caq, cak, cav, lam))
    ob_c = gla_out(ofc + obc, pc[7])
    oc_c = _window_attention_ctx(ccq, cck, ccv, swa_sink)
    return out_x, merge(pc[11], oa_c, ob_c, oc_c)


def setup_inputs(seed: int = 0) -> dict:
    key = jax.random.key(seed)
    ks = jax.random.split(key, 24)
    f32 = jnp.float32
    D = D_MODEL
    L = DEPTH

    def nrm(k, shape, scale):
        return jax.random.normal(k, shape, f32) * scale

    return {
        'x': nrm(ks[0], (BATCH, SEQ, D), 1.0),
        'c': nrm(ks[1], (BATCH, D), 1.0),
        'ctx': nrm(ks[2], (BATCH, CTX_LEN, D), 1.0),
        'c_ctx': nrm(ks[3], (D,), 1.0),
        'w_ada': nrm(ks[4], (L, D, N_MOD * D), 0.3 * D ** -0.5),
        'b_ada': nrm(ks[5], (L, N_MOD * D), 0.02),
        'norm_g': 1.0 + nrm(ks[6], (L, 3, D), 0.02),
        'w_ffn1_in': nrm(ks[7], (L, D, 2 * FFN_DIM), D ** -0.5),
        'w_ffn1_out': nrm(ks[8], (L, FFN_DIM, D), FFN_DIM ** -0.5),
        'w_ffn2_in': nrm(ks[9], (L, D, 2 * FFN_DIM), D ** -0.5),
        'w_ffn2_out': nrm(ks[10], (L, FFN_DIM, D), FFN_DIM ** -0.5),
        'w_mix_in': nrm(ks[11], (L, D, IN_COLS), D ** -0.5),
        'diff_lambda': nrm(ks[12], (L, 4, HEAD_DIM), 0.1),
        'diff_subln': 1.0 + nrm(ks[13], (L, A_VDIM), 0.02),
        'gla_gate_w': nrm(ks[14], (L, 2, GATE_RANK, B_HEADS * B_KDIM), GATE_RANK ** -0.5),
        'gla_gate_b': nrm(ks[15], (L, 2, B_HEADS * B_KDIM), 0.1),
        'gla_norm': 1.0 + nrm(ks[16], (L, B_VDIM), 0.02),
        'swa_sink': nrm(ks[17], (L, C_HEADS), 0.5),
        'w_br_a': nrm(ks[18], (L, A_HEADS * A_VDIM, D), (A_HEADS * A_VDIM) ** -0.5),
        'w_br_b': nrm(ks[19], (L, B_HEADS * B_VDIM, D), (B_HEADS * B_VDIM) ** -0.5),
        'w_br_c': nrm(ks[20], (L, C_HEADS * HEAD_DIM, D), (C_HEADS * HEAD_DIM) ** -0.5),
        'w_mix_out': nrm(ks[21], (L, D, D), D ** -0.5),
        'final_g': 1.0 + nrm(ks[22], (D,), 0.02),
    }


def reference(x, c, ctx, c_ctx, w_ada, b_ada, norm_g, w_ffn1_in, w_ffn1_out, w_ffn2_in, w_ffn2_out,
              w_mix_in, diff_lambda, diff_subln, gla_gate_w, gla_gate_b, gla_norm, swa_sink,
              w_br_a, w_br_b, w_br_c, w_mix_out, final_g):
    bsz, n, d = x.shape
    cos, sin = _rope_tables(n)
    sc = jax.nn.silu(c)
    scc = jax.nn.silu(c_ctx)
    for l in range(DEPTH):
        with_ctx = l < DEPTH - 1
        lam_init = 0.8 - 0.6 * math.exp(-0.3 * l)
        mx = (sc @ w_ada[l] + b_ada[l]).reshape(bsz, N_MOD, 1, d)
        mc = (scc @ w_ada[l] + b_ada[l]).reshape(N_MOD, d)
        mx = [mx[:, i] for i in range(N_MOD)]
        mc = [mc[i] for i in range(N_MOD)]
        x = x + 0.5 * mx[2] * _swiglu(_modulate(_rmsnorm(x, norm_g[l, 0]), mx[0], mx[1]), w_ffn1_in[l], w_ffn1_out[l])
        ctx = ctx + 0.5 * mc[2] * _swiglu(_modulate(_rmsnorm(ctx, norm_g[l, 0]), mc[0], mc[1]), w_ffn1_in[l], w_ffn1_out[l])
        hx = _modulate(_rmsnorm(x, norm_g[l, 1]), mx[3], mx[4])
        hc = _modulate(_rmsnorm(ctx, norm_g[l, 1]), mc[3], mc[4])
        ox, oc = _token_mix(hx, hc, w_mix_in[l], diff_lambda[l], diff_subln[l], gla_gate_w[l], gla_gate_b[l],
                            gla_norm[l], swa_sink[l], w_br_a[l], w_br_b[l], w_br_c[l], w_mix_out[l],
                            lam_init, cos, sin, with_ctx)
        x = x + mx[5] * ox
        x = x + 0.5 * mx[8] * _swiglu(_modulate(_rmsnorm(x, norm_g[l, 2]), mx[6], mx[7]), w_ffn2_in[l], w_ffn2_out[l])
        if with_ctx:
            ctx = ctx + mc[5] * oc
            ctx = ctx + 0.5 * mc[8] * _swiglu(_modulate(_rmsnorm(ctx, norm_g[l, 2]), mc[6], mc[7]), w_ffn2_in[l], w_ffn2_out[l])
    return _rmsnorm(x, final_g)
```

```python
import math
from contextlib import ExitStack
import numpy as np
import concourse.bass as bass
import concourse.mybir as mybir
from concourse.bass_utils import run_bass_kernel_spmd

F32 = mybir.dt.float32
BF16 = mybir.dt.bfloat16
ALU = mybir.AluOpType
AF = mybir.ActivationFunctionType

D = 1024
SEQ = 4096
CTX = 256
NT = SEQ + CTX
L_ALL = 4
FFN = 2816
INC = 6944
EPS = 1e-6
O_AQ, O_AK, O_AV = 0, 512, 1024
O_BQ, O_BK, O_BV, O_LR, O_R = 1536, 1792, 2048, 2560, 2592
O_CQ, O_CK, O_CV, O_G = 3104, 3616, 3744, 3872
P_AQ, P_AK, P_CQ, P_CK = 0, 512, 1024, 1536
NPERM = 1664


class Buf:
    __slots__ = ("name", "w", "r")

    def __init__(self, name=""):
        self.name = name
        self.w = {}
        self.r = {}


class Eng:
    def __init__(self, fw, name, handle):
        self.name = name
        self.h = handle
        self.sem = fw.new_sem("e_" + name)
        self.cnt = 0
        self.known = {}

    def wait_tok(self, sem, val):
        k = id(sem)
        if self.known.get(k, 0) >= val:
            return
        self.h.wait_ge(sem, val)
        self.known[k] = val


class T:
    __slots__ = ("t", "b")

    def __init__(self, t, b):
        self.t = t
        self.b = b

    def __getitem__(self, idx):
        return self.t[idx]


class Ring:
    def __init__(self, items):
        self.items = items
        self.i = 0

    def next(self):
        t = self.items[self.i]
        self.i = (self.i + 1) % len(self.items)
        return t


class FW:
    def __init__(self, nc, n_dma_sems=16):
        self.nc = nc
        self.stacks = [ExitStack()]
        self.pe = Eng(self, "pe", nc.tensor)
        self.dve = Eng(self, "dve", nc.vector)
        self.act = Eng(self, "act", nc.scalar)
        self.pool = Eng(self, "pool", nc.gpsimd)
        self.sp = Eng(self, "sp", nc.sync)
        self.engs = [self.pe, self.dve, self.act, self.pool, self.sp]
        self.dsems = {}
        for e in (self.sp, self.pool, self.act):
            self.dsems[e.name] = [[self.new_sem("d_%s%d" % (e.name, i)), 0] for i in range(n_dma_sems)]
        self.dnext = {e: 0 for e in self.dsems}
        self.bar_sem = self.new_sem("bar")
        self.bar_cnt = 0
        self.n_inst = 0
        self.uid = 0

    def new_sem(self, name):
        return self.stacks[0].enter_context(self.nc.semaphore(name))

    def sbuf(self, name, shape, dtype):
        self.uid += 1
        t = self.stacks[-1].enter_context(self.nc.sbuf_tensor("%s_%d" % (name, self.uid), list(shape), dtype))
        return T(t, Buf(name))

    def psum(self, name, shape, dtype=F32):
        t = self.stacks[-1].enter_context(self.nc.psum_tensor(name, list(shape), dtype))
        return T(t, Buf(name))

    def ring(self, name, shape, dtype, n):
        return Ring([self.sbuf("%s%d" % (name, i), shape, dtype) for i in range(n)])

    def _deps(self, eng, reads, writes, self_sync):
        for b in reads:
            for (sem, val) in b.w.values():
                if sem is eng.sem and not self_sync:
                    continue
                eng.wait_tok(sem, val)
        for b in writes:
            for (sem, val) in b.w.values():
                if sem is eng.sem and not self_sync:
                    continue
                eng.wait_tok(sem, val)
            for (sem, val) in b.r.values():
                if sem is eng.sem and not self_sync:
                    continue
                eng.wait_tok(sem, val)

    def op(self, eng, fn, reads, writes, self_sync=True):
        self._deps(eng, reads, writes, self_sync)
        ins = fn()
        self.n_inst += 1
        eng.cnt += 1
        ins.then_inc(eng.sem, 1)
        tok = (eng.sem, eng.cnt)
        k = id(eng.sem)
        for b in reads:
            b.r[k] = tok
        for b in writes:
            b.w[k] = tok
            b.r = {}
        return ins

    def dma(self, eng, out_ap, in_ap, reads, writes):
        pool = self.dsems[eng.name]
        i = self.dnext[eng.name]
        self.dnext[eng.name] = (i + 1) % len(pool)
        slot = pool[i]
        sem = slot[0]
        if slot[1] > 0:
            eng.wait_tok(sem, slot[1])
        self._deps(eng, reads, writes, True)
        ins = eng.h.dma_start(out=out_ap, in_=in_ap)
        self.n_inst += 1
        slot[1] += 16
        ins.then_inc(sem, 16)
        tok = (sem, slot[1])
        k = id(sem)
        for b in reads:
            b.r[k] = tok
        for b in writes:
            b.w[k] = tok
            b.r = {}
        return ins

    def barrier(self):
        sp = self.sp
        for e in self.engs:
            if e is not sp and e.cnt > 0:
                sp.wait_tok(e.sem, e.cnt)
        for pool in self.dsems.values():
            for (sem, val) in pool:
                if val > 0:
                    sp.wait_tok(sem, val)
        self.bar_cnt += 1
        sp.h.sem_inc(self.bar_sem, 1)
        for e in self.engs:
            if e is not sp:
                e.wait_tok(self.bar_sem, self.bar_cnt)
        for e in self.engs:
            for e2 in self.engs:
                e.known[id(e2.sem)] = e2.cnt
            for pool in self.dsems.values():
                for (sem, val) in pool:
                    e.known[id(sem)] = val

    def phase(self):
        fw = self

        class _P:
            def __enter__(s):
                fw.stacks.append(ExitStack())

            def __exit__(s, *a):
                if a[0] is None:
                    fw.barrier()
                st = fw.stacks.pop()
                st.close()
                return False
        return _P()

    def close(self):
        self.stacks[0].close()


class DT:
    def __init__(self, nc, name, rows, cols, dtype, kind="Internal", rb=128, cb=128):
        self.ap = nc.dram_tensor(name, [rows, cols], dtype, kind=kind).ap()
        self.rb, self.cb = rb, cb
        self.bufs = {}
        self.name = name

    def B(self, r0, r1, c0, c1):
        out = []
        for i in range(r0 // self.rb, (r1 - 1) // self.rb + 1):
            for j in range(c0 // self.cb, (c1 - 1) // self.cb + 1):
                b = self.bufs.get((i, j))
                if b is None:
                    b = self.bufs[(i, j)] = Buf(self.name)
                out.append(b)
        return out


STS = [(0, 1024), (1024, 1024), (2048, 1024), (3072, 1024), (4096, 256)]


def lam_init_of(l):
    return 0.8 - 0.6 * math.exp(-0.3 * l)


B2CUT = [99]


def build(n_layers=L_ALL, stop_after=None, dump=None, mixers="ABC"):
    nc = bass.Bass("TRN2", target_bir_lowering=False)
    fw = FW(nc)
    V, A, G, PE, SP = fw.dve, fw.act, fw.pool, fw.pe, fw.sp
    nv, na, ng, nt = nc.vector, nc.scalar, nc.gpsimd, nc.tensor

    def din(name, shape, dtype=F32):
        return nc.dram_tensor(name, list(shape), dtype, kind="ExternalInput").ap()

    xc_in = din("xc", [D, NT])
    scT_in = din("scT", [128, 8, 2])
    w_ada = din("w_ada", [L_ALL, D, 9 * D])
    b_adaT = din("b_adaT", [128, L_ALL, 72])
    normgT = din("normgT", [128, L_ALL, 3, 8])
    fgT = din("fgT", [128, 8])
    w_ffn_in = [din("w_ffn1_in", [L_ALL, D, 2 * FFN]), din("w_ffn2_in", [L_ALL, D, 2 * FFN])]
    w_ffn_out = [din("w_ffn1_out", [L_ALL, FFN, D]), din("w_ffn2_out", [L_ALL, FFN, D])]
    w_mix_in = din("w_mix_in", [L_ALL, D, INC])
    w_br = [din("w_br_a", [L_ALL, 512, D]), din("w_br_b", [L_ALL, 512, D]), din("w_br_c", [L_ALL, 512, D])]
    w_mix_out = din("w_mix_out", [L_ALL, D, D])
    ropeC = din("ropeC", [128, NT])
    ropeS = din("ropeS", [128, NT])
    dlam = din("dlam", [128, L_ALL, 4, 64])
    sublnT = din("sublnT", [128, L_ALL])
    gnT = din("gnT", [128, L_ALL])
    gw2_in = din("gw2", [L_ALL, 64, 512])
    sinkB = din("sinkB", [128, L_ALL, 8])
    tri_in = din("tri", [128, 4, 128])
    permT_in = din("permT", [128, 128])
    out_dt = DT(nc, "out", D, SEQ, F32, kind="ExternalOutput")

    xs = DT(nc, "xs", D, NT, F32)
    h2s = DT(nc, "h2s", D, NT, BF16)
    Aq = DT(nc, "Aq", 512, NT, BF16); Ak = DT(nc, "Ak", 512, NT, BF16); Av = DT(nc, "Av", NT, 512, BF16)
    Bq = DT(nc, "Bq", 256, NT, BF16); Bk = DT(nc, "Bk", 256, NT, BF16); BkT = DT(nc, "BkT", NT, 256, BF16)
    Bv = DT(nc, "Bv", NT, 512, BF16); Blr = DT(nc, "Blr", 32, NT, BF16, rb=32); Br = DT(nc, "Br", 512, NT, BF16)
    Cq = DT(nc, "Cq", 512, NT, BF16); Ck = DT(nc, "Ck", 128, NT, BF16); Cv = DT(nc, "Cv", NT, 128, BF16)
    Oa = DT(nc, "Oa", 512, NT, BF16); Ob = DT(nc, "Ob", 512, NT, BF16); Oc = DT(nc, "Oc", 512, NT, BF16, rb=64)

    def wscr(name, shape):
        return (nc.dram_tensor(name, list(shape), BF16, kind="Internal").ap(), Buf(name))
    WB = []
    for l in range(n_layers):
        WB.append(dict(
            f_in=[wscr("wb_f1i_%d" % l, [D, 2 * FFN]), wscr("wb_f2i_%d" % l, [D, 2 * FFN])],
            f_out=[wscr("wb_f1o_%d" % l, [FFN, D]), wscr("wb_f2o_%d" % l, [FFN, D])],
            mix=wscr("wb_mix_%d" % l, [D, INC]),
            br=[wscr("wb_br%d_%d" % (i, l), [512, D]) for i in range(3)],
            out=wscr("wb_out_%d" % l, [D, D])))

    def convert_layer(l, part, after=()):
        w = WB[l]
        if part == 0:
            fw.dma(G, w["f_in"][0][0], w_ffn_in[0][l], list(after), [w["f_in"][0][1]])
            fw.dma(G, w["f_out"][0][0], w_ffn_out[0][l], list(after), [w["f_out"][0][1]])
            return
        fw.dma(G, w["mix"][0], w_mix_in[l], list(after), [w["mix"][1]])
        for i in range(3):
            fw.dma(G, w["br"][i][0], w_br[i][l], list(after), [w["br"][i][1]])
        fw.dma(G, w["out"][0], w_mix_out[l], list(after), [w["out"][1]])
        fw.dma(G, w["f_in"][1][0], w_ffn_in[1][l], list(after), [w["f_in"][1][1]])
        fw.dma(G, w["f_out"][1][0], w_ffn_out[1][l], list(after), [w["f_out"][1][1]])

    ones_bf = fw.sbuf("ones_bf", [128, 128], BF16)
    ones_f = fw.sbuf("ones_f", [128, 2], F32)
    tri_f = fw.sbuf("tri_f", [128, 4, 128], F32)
    tri_b = fw.sbuf("tri_b", [128, 4, 128], BF16)
    modT = fw.sbuf("modT", [128, L_ALL, 9, 8, 2], F32)
    gsT = fw.sbuf("gsT", [128, L_ALL, 3, 8, 2], F32)
    cfT = fw.sbuf("cfT", [128, L_ALL, 3, 8, 2], F32)
    ngT = fw.sbuf("ngT", [128, L_ALL, 3, 8], F32)
    fg = fw.sbuf("fg", [128, 8], F32)
    neglam = fw.sbuf("neglam", [128, L_ALL], F32)
    subl = fw.sbuf("subl", [128, L_ALL], F32)
    gn = fw.sbuf("gn", [128, L_ALL], F32)
    esink = fw.sbuf("esink", [128, L_ALL, 8], F32)
    PSB = [fw.psum("ps%d" % i, [128, 512], F32) for i in range(8)]
    PS = Ring(PSB)

    fw.op(V, lambda: nv.memset(ones_bf[:], 1.0), [], [ones_bf.b])
    fw.op(V, lambda: nv.memset(ones_f[:], 1.0), [], [ones_f.b])
    ones_ff = fw.sbuf("ones_ff", [128, 128], F32)
    fw.op(V, lambda: nv.memset(ones_ff[:], 1.0), [], [ones_ff.b])
    eps_t = fw.sbuf("eps_t", [128, 2], F32)
    fw.op(V, lambda: nv.memset(eps_t[:], EPS), [], [eps_t.b])
    fw.dma(SP, tri_f[:], tri_in, [], [tri_f.b])
    fw.dma(G, tri_b[:], tri_in, [], [tri_b.b])
    permT = fw.sbuf("permT", [128, 128], BF16)
    fw.dma(G, permT[:], permT_in, [], [permT.b])
    fw.dma(SP, ngT[:], normgT, [], [ngT.b])
    fw.dma(SP, fg[:], fgT, [], [fg.b])
    fw.dma(SP, subl[:], sublnT, [], [subl.b])
    fw.dma(SP, gn[:], gnT, [], [gn.b])
    xin_b = Buf("xin")
    for (t0, n) in STS:
        fw.dma(SP, xs.ap[:, t0:t0 + n], xc_in[:, t0:t0 + n], [], xs.B(0, D, t0, t0 + n))
    convert_layer(0, 0)
    convert_layer(0, 1)

    with fw.phase():
        sc = fw.sbuf("sc", [128, 8, 2], F32)
        sg = fw.sbuf("sg", [128, 8, 2], F32)
        badd = fw.sbuf("badd", [128, L_ALL, 72], F32)
        fw.dma(SP, sc[:], scT_in, [], [sc.b])
        fw.dma(SP, badd[:], b_adaT, [], [badd.b])
        fw.op(A, lambda: na.activation(out=sg[:], in_=sc[:], func=AF.Sigmoid), [sc.b], [sg.b])
        fw.op(V, lambda: nv.tensor_tensor(out=sc[:], in0=sc[:], in1=sg[:], op=ALU.mult), [sc.b, sg.b], [sc.b])
        wa_r = fw.ring("wa", [128, 4608], F32, 3)
        for l in range(n_layers):
            for half in range(2):
                acc = PS.next()
                for kc in range(8):
                    wt = wa_r.next()
                    fw.dma(SP, wt[:], w_ada[l, kc * 128:(kc + 1) * 128, half * 4608:(half + 1) * 4608], [], [wt.b])
                    for j in range(36):
                        fw.op(PE, lambda: nt.matmul(acc[:, 2 * j:2 * j + 2], lhsT=wt[:, j * 128:(j + 1) * 128],
                                                    rhs=sc[:, kc, :], start=(kc == 0 and j == 0), stop=(kc == 7),
                                                    skip_group_check=True),
                              [wt.b, sc.b], [acc.b], self_sync=False)
                mv = modT[:, l].rearrange("p i c w -> p (i c) w")[:, half * 36:(half + 1) * 36, :]
                fw.op(V, lambda: nv.tensor_tensor(
                    out=mv, in0=acc[:, 0:72].rearrange("p (j w) -> p j w", w=2),
                    in1=badd[:, l, half * 36:(half + 1) * 36].unsqueeze(2).to_broadcast([128, 36, 2]), op=ALU.add),
                    [acc.b, badd.b], [modT.b])
            for i in range(3):
                fw.op(V, lambda: nv.scalar_tensor_tensor(
                    out=gsT[:, l, i], in0=modT[:, l, 3 * i + 1], scalar=1.0,
                    in1=ngT[:, l, i].unsqueeze(2).to_broadcast([128, 8, 2]), op0=ALU.add, op1=ALU.mult),
                    [modT.b, ngT.b], [gsT.b])
                fw.op(V, lambda: nv.tensor_scalar(out=cfT[:, l, i], in0=modT[:, l, 3 * i + 2],
                                                  scalar1=(1.0 if i == 1 else 0.5), scalar2=None, op0=ALU.mult),
                      [modT.b], [cfT.b])
        dl = fw.sbuf("dl", [128, L_ALL, 4, 64], F32)
        pr = fw.sbuf("pr", [128, L_ALL, 2, 64], F32)
        sm = fw.sbuf("sm", [128, L_ALL, 2], F32)
        fw.dma(SP, dl[:], dlam, [], [dl.b])
        for l in range(n_layers):
            for j in range(2):
                fw.op(V, lambda: nv.tensor_tensor(out=pr[:, l, j], in0=dl[:, l, 2 * j], in1=dl[:, l, 2 * j + 1], op=ALU.mult),
                      [dl.b], [pr.b])
                fw.op(V, lambda: nv.reduce_sum(out=sm[:, l, j:j + 1], in_=pr[:, l, j], axis=mybir.AxisListType.X),
                      [pr.b], [sm.b])
        fw.op(A, lambda: na.activation(out=sm[:], in_=sm[:], func=AF.Exp), [sm.b], [sm.b])
        for l in range(n_layers):
            fw.op(V, lambda: nv.tensor_tensor(out=neglam[:, l:l + 1], in0=sm[:, l, 1:2], in1=sm[:, l, 0:1], op=ALU.subtract),
                  [sm.b], [neglam.b])
            fw.op(V, lambda: nv.tensor_scalar(out=neglam[:, l:l + 1], in0=neglam[:, l:l + 1], scalar1=-lam_init_of(l),
                                              scalar2=None, op0=ALU.add), [neglam.b], [neglam.b])
            fw.op(V, lambda: nv.tensor_scalar(out=subl[:, l:l + 1], in0=subl[:, l:l + 1], scalar1=1.0 - lam_init_of(l),
                                              scalar2=None, op0=ALU.mult), [subl.b], [subl.b])
        sk = fw.sbuf("sk", [128, L_ALL, 8], F32)
        fw.dma(SP, sk[:], sinkB, [], [sk.b])
        fw.op(A, lambda: na.activation(out=esink[:], in_=sk[:], func=AF.Exp), [sk.b], [esink.b])

    def mcol(t, l, i, kc, isctx):
        w = 1 if isctx else 0
        return t[:, l, i, kc, w:w + 1]

    def rstd_from(rs, acc, n, inv_cnt, np_=128):
        fw.op(V, lambda: nv.tensor_scalar(out=rs[0:np_, 0:n], in0=acc[0:np_, 0:n], scalar1=inv_cnt, scalar2=EPS,
                                          op0=ALU.mult, op1=ALU.add), [acc.b], [rs.b])
        fw.op(A, lambda: na.activation(out=rs[0:np_, 0:n], in_=rs[0:np_, 0:n], func=AF.Sqrt), [rs.b], [rs.b])
        fw.op(V, lambda: nv.reciprocal(out=rs[0:np_, 0:n], in_=rs[0:np_, 0:n]), [rs.b], [rs.b])

    def norm_mod(l, ni, src, t0, n, isctx, h, hoff, rings, plain_scale=None):
        xt_r, sq_r, rs_r, tm_r = rings
        xt = xt_r.next()
        fw.dma(SP, xt[:, :, 0:n], src.ap[:, t0:t0 + n].rearrange("(k p) n -> p k n", p=128), src.B(0, D, t0, t0 + n), [xt.b])
        acc = PS.next()
        for kc in range(8):
            sq = sq_r.next()
            fw.op(A, lambda: na.activation(out=sq[:, 0:n], in_=xt[:, kc, 0:n], func=AF.Square), [xt.b], [sq.b])
            fw.op(PE, lambda: nt.matmul(acc[:, 0:n], lhsT=ones_bf[:], rhs=sq[:, 0:n], start=(kc == 0), stop=(kc == 7)),
                  [sq.b, ones_bf.b], [acc.b], self_sync=False)
        rs = rs_r.next()
        rstd_from(rs, acc, n, 1.0 / D)
        return xt, rs

    def apply_mod(l, ni, xt, rs, n, isctx, h, hoff, tm_r):
        for kc in range(8):
            tm = tm_r.next()
            fw.op(V, lambda: nv.scalar_tensor_tensor(out=tm[:, 0:n], in0=xt[:, kc, 0:n], scalar=mcol(gsT, l, ni, kc, isctx),
                                                     in1=rs[:, 0:n], op0=ALU.mult, op1=ALU.mult),
                  [xt.b, rs.b, gsT.b], [tm.b])
            fw.op(A, lambda: na.activation(out=h[:, kc, hoff:hoff + n], in_=tm[:, 0:n], func=AF.Identity,
                                           bias=modT[:, l, 3 * ni, kc, (1 if isctx else 0):(2 if isctx else 1)]),
                  [tm.b, modT.b], [h.b])

    def ld_w(eng, wt_ap, wsrc, rows0, nk, c0, ncols, wt_T):
        ap, b = wsrc
        fw.dma(eng, wt_ap, ap[rows0:rows0 + nk * 128, c0:c0 + ncols].rearrange("(k p) c -> p k c", p=128), [b], [wt_T.b])

    def residual_out(l, ci, pt, oc, t0, n, isctx, xr_r, xo_r, dst=xs):
        xr = xr_r.next()
        fw.dma(SP, xr[:, 0:n], xs.ap[oc * 128:(oc + 1) * 128, t0:t0 + n], xs.B(oc * 128, (oc + 1) * 128, t0, t0 + n), [xr.b])
        xo = xo_r.next()
        fw.op(V, lambda: nv.scalar_tensor_tensor(out=xo[:, 0:n], in0=pt[:, 0:n], scalar=mcol(cfT, l, ci, oc, isctx),
                                                 in1=xr[:, 0:n], op0=ALU.mult, op1=ALU.add),
              [pt.b, xr.b, cfT.b], [xo.b])
        fw.dma(SP, xs.ap[oc * 128:(oc + 1) * 128, t0:t0 + n], xo[:, 0:n], [xo.b], xs.B(oc * 128, (oc + 1) * 128, t0, t0 + n))

    def tiles_of(t0, n):
        return [(t0 + i, min(512, n - i)) for i in range(0, n, 512)]

    def ffn_phase(l, which, skip_ctx=False):
        ni = 0 if which == 0 else 2
        W1 = WB[l]["f_in"][which]
        W2 = WB[l]["f_out"][which]
        with fw.phase():
            h = fw.sbuf("h", [128, 8, 1024], BF16)
            g = fw.sbuf("g", [128, 22, 1024], BF16)
            rings = (fw.ring("xt", [128, 8, 256], F32, 2), fw.ring("sq", [128, 256], BF16, 3),
                     fw.ring("rs", [128, 256], F32, 2), fw.ring("tm", [128, 256], F32, 3))
            wu_r = fw.ring("wu", [128, 8, 2, 512], BF16, 2)
            w2_r = fw.ring("w2", [128, 22, 256], BF16, 2)
            s_r = fw.ring("s", [128, 512], F32, 3)
            xr_r = fw.ring("xr", [128, 512], F32, 3)
            xo_r = fw.ring("xo", [128, 512], F32, 3)
            sts = [(t0, n) for (t0, n) in STS if not (t0 >= SEQ and skip_ctx)]

            def do_norm(t0, n):
                for off in range(0, n, 256):
                    xt, rs = norm_mod(l, ni, xs, t0 + off, 256, t0 >= SEQ, h, off, rings)
                    apply_mod(l, ni, xt, rs, 256, t0 >= SEQ, h, off, rings[3])

            def ld_w1(j0):
                nj = min(4, 22 - j0)
                wt = wu_r.next()
                ld_w(SP, wt[:, :, 0, 0:nj * 128], W1, 0, 8, j0 * 128, nj * 128, wt)
                ld_w(SP, wt[:, :, 1, 0:nj * 128], W1, 0, 8, FFN + j0 * 128, nj * 128, wt)
                return wt

            def ld_w2(op2):
                w2 = w2_r.next()
                ld_w(SP, w2[:], W2, 0, 22, op2 * 256, 256, w2)
                return w2

            do_norm(*sts[0])
            for si, (t0, n) in enumerate(sts):
                isctx = t0 >= SEQ
                tls = tiles_of(0, n)
                groups = list(range(0, 22, 4))
                if si == 0:
                    nxt_w = ld_w1(groups[0])
                for gi_, j0 in enumerate(groups):
                    nj = min(4, 22 - j0)
                    wt = nxt_w
                    if gi_ + 1 < len(groups):
                        nxt_w = ld_w1(groups[gi_ + 1])
                    else:
                        nxt_w2 = ld_w2(0)
                    for jj in range(nj):
                        for (o, m) in tls:
                            pu = PS.next()
                            for kc in range(8):
                                fw.op(PE, lambda: nt.matmul(pu[:, 0:m], lhsT=wt[:, kc, 0, jj * 128:(jj + 1) * 128], rhs=h[:, kc, o:o + m],
                                                            start=(kc == 0), stop=(kc == 7)), [wt.b, h.b], [pu.b], self_sync=False)
                            pv = PS.next()
                            for kc in range(8):
                                fw.op(PE, lambda: nt.matmul(pv[:, 0:m], lhsT=wt[:, kc, 1, jj * 128:(jj + 1) * 128], rhs=h[:, kc, o:o + m],
                                                            start=(kc == 0), stop=(kc == 7)), [wt.b, h.b], [pv.b], self_sync=False)
                            s = s_r.next()
                            fw.op(A, lambda: na.activation(out=s[:, 0:m], in_=pu[:, 0:m], func=AF.Silu), [pu.b], [s.b])
                            fw.op(V, lambda: nv.tensor_tensor(out=g[:, j0 + jj, o:o + m], in0=s[:, 0:m], in1=pv[:, 0:m], op=ALU.mult),
                                  [s.b, pv.b], [g.b])
                for op2 in range(4):
                    w2 = nxt_w2
                    if op2 + 1 < 4:
                        nxt_w2 = ld_w2(op2 + 1)
                    elif si + 1 < len(sts):
                        nxt_w = ld_w1(groups[0])
                    if op2 == 1 and si + 1 < len(sts):
                        do_norm(*sts[si + 1])
                    for oo in range(2):
                        oc = op2 * 2 + oo
                        for (o, m) in tls:
                            pt = PS.next()
                            for j in range(22):
                                fw.op(PE, lambda: nt.matmul(pt[:, 0:m], lhsT=w2[:, j, oo * 128:(oo + 1) * 128], rhs=g[:, j, o:o + m],
                                                            start=(j == 0), stop=(j == 21)), [w2.b, g.b], [pt.b], self_sync=False)
                            residual_out(l, 0 if which == 0 else 2, pt, oc, t0 + o, m, isctx, xr_r, xo_r)

    def mixin_phase(l):
        Wm = WB[l]["mix"]
        with fw.phase():
            h = fw.sbuf("h2", [128, 8, 1024], BF16)
            rings = (fw.ring("xt", [128, 8, 256], F32, 2), fw.ring("sq", [128, 256], BF16, 3),
                     fw.ring("rs", [128, 256], F32, 2), fw.ring("tm", [128, 256], F32, 3))
            w_r = fw.ring("wm", [128, 8, 512], BF16, 3)
            qsb_r = fw.ring("qsb", [128, 512], BF16, 4)
            rc = fw.sbuf("rc", [128, 1024], F32)
            rsn = fw.sbuf("rsn", [128, 1024], F32)
            t1_r = fw.ring("t1", [128, 512], F32, 2)
            t2_r = fw.ring("t2", [128, 512], F32, 2)
            ob_r = fw.ring("ob", [128, 512], BF16, 6)
            FMS = [("rope", O_AQ, P_AQ, 4, Aq), ("rope", O_AK, P_AK, 4, Ak), ("plain", O_BQ, None, 2, Bq),
                   ("plain", O_BK, None, 2, Bk), ("silu", O_R, None, 4, Br), ("rope", O_CQ, P_CQ, 4, Cq),
                   ("rope", O_CK, P_CK, 1, Ck)]
            TMS = [(O_AV, 512, Av), (O_BK, 256, BkT), (O_BV, 512, Bv), (O_CV, 128, Cv)]
            flip = [0]
            rope_pend = []
            h_r = Ring([h, fw.sbuf("h2b", [128, 8, 1024], BF16)])

            def do_norm(t0, n, hh_):
                for off in range(0, n, 256):
                    xt, rs = norm_mod(l, 1, xs, t0 + off, 256, t0 >= SEQ, hh_, off, rings)
                    apply_mod(l, 1, xt, rs, 256, t0 >= SEQ, hh_, off, rings[3])

            def ld_seg(seg):
                if seg[0] == "fm":
                    (kind, c0, pc0, nch, dst) = seg[1]
                    wt = w_r.next()
                    ld_w(SP, wt[:, :, 0:nch * 128], Wm, 0, 8, c0, nch * 128, wt)
                    return (wt, None)
                if seg[0] == "lr":
                    wt = w_r.next()
                    ld_w(SP, wt[:, :, 0:32], Wm, 0, 8, O_LR, 32, wt)
                    return (wt, None)
                (c0, ncols, dst) = seg[1]
                wt = w_r.next()
                ld_w(SP, wt[:, :, 0:ncols], Wm, 0, 8, c0, ncols, wt)
                return (wt, None)

            segs = [("fm", f_) for f_ in FMS] + [("lr", None)] + [("tm", t_) for t_ in TMS]
            h = h_r.next()
            do_norm(STS[0][0], STS[0][1], h)
            for si, (t0, n) in enumerate(STS):
                isctx = t0 >= SEQ
                h_next = None
                fw.dma(SP, h2s.ap[:, t0:t0 + n].rearrange("(k p) n -> p k n", p=128), h[:, :, 0:n], [h.b], h2s.B(0, D, t0, t0 + n))
                fw.dma(SP, rc[:, 0:n], ropeC[:, t0:t0 + n], [], [rc.b])
                fw.dma(SP, rsn[:, 0:n], ropeS[:, t0:t0 + n], [], [rsn.b])
                tls = tiles_of(0, n)
                if si == 0:
                    nxt = ld_seg(segs[0])
                for k, seg in enumerate(segs):
                    (wt, wp) = nxt
                    if k + 1 < len(segs):
                        nxt = ld_seg(segs[k + 1])
                    elif si + 1 < len(STS):
                        nxt = ld_seg(segs[0])
                    if k == 5 and si + 1 < len(STS):
                        h_next = h_r.next()
                        do_norm(STS[si + 1][0], STS[si + 1][1], h_next)
                    if seg[0] == "fm":
                        (kind, c0, pc0, nch, dst) = seg[1]
                        for j in range(nch):
                            for (o, m) in tls:
                                p1 = PS.next()
                                for kc in range(8):
                                    fw.op(PE, lambda: nt.matmul(p1[:, 0:m], lhsT=wt[:, kc, j * 128:(j + 1) * 128], rhs=h[:, kc, o:o + m],
                                                                start=(kc == 0), stop=(kc == 7)), [wt.b, h.b], [p1.b], self_sync=False)
                                ob = ob_r.next()
                                if kind == "rope":
                                    qsb = qsb_r.next()
                                    fw.op(A, lambda: na.activation(out=qsb[:, 0:m], in_=p1[:, 0:m], func=AF.Copy), [p1.b], [qsb.b])

                                    def rope_tail(p1=p1, qsb=qsb, ob=ob, o=o, m=m, j=j, dst=dst):
                                        p2 = PS.next()
                                        fw.op(PE, lambda: nt.matmul(p2[:, 0:m], lhsT=permT[:], rhs=qsb[:, 0:m], start=True, stop=True),
                                              [permT.b, qsb.b], [p2.b], self_sync=False)
                                        t1 = t1_r.next()
                                        t2 = t2_r.next()
                                        fw.op(V, lambda: nv.tensor_tensor(out=t1[:, 0:m], in0=p1[:, 0:m], in1=rc[:, o:o + m], op=ALU.mult),
                                              [p1.b, rc.b, qsb.b], [t1.b])
                                        fw.op(V, lambda: nv.tensor_tensor(out=t2[:, 0:m], in0=p2[:, 0:m], in1=rsn[:, o:o + m], op=ALU.mult),
                                              [p2.b, rsn.b], [t2.b])
                                        fw.op(G, lambda: ng.tensor_tensor(out=ob[:, 0:m], in0=t1[:, 0:m], in1=t2[:, 0:m], op=ALU.add),
                                              [t1.b, t2.b], [ob.b])
                                        fw.dma(SP, dst.ap[j * 128:(j + 1) * 128, t0 + o:t0 + o + m], ob[:, 0:m], [ob.b],
                                               dst.B(j * 128, (j + 1) * 128, t0 + o, t0 + o + m))
                                    if rope_pend:
                                        rope_pend.pop(0)()
                                    rope_pend.append(rope_tail)
                                    continue
                                elif kind == "silu":
                                    fw.op(A, lambda: na.activation(out=ob[:, 0:m], in_=p1[:, 0:m], func=AF.Silu), [p1.b], [ob.b])
                                else:
                                    fw.op(A, lambda: na.activation(out=ob[:, 0:m], in_=p1[:, 0:m], func=AF.Copy), [p1.b], [ob.b])
                                fw.dma(SP, dst.ap[j * 128:(j + 1) * 128, t0 + o:t0 + o + m], ob[:, 0:m], [ob.b],
                                       dst.B(j * 128, (j + 1) * 128, t0 + o, t0 + o + m))
                        while rope_pend:
                            rope_pend.pop(0)()
                    elif seg[0] == "lr":
                        for (o, m) in tls:
                            p1 = PS.next()
                            for kc in range(8):
                                fw.op(PE, lambda: nt.matmul(p1[0:32, 0:m], lhsT=wt[:, kc, 0:32], rhs=h[:, kc, o:o + m],
                                                            start=(kc == 0), stop=(kc == 7)), [wt.b, h.b], [p1.b], self_sync=False)
                            ob = ob_r.next()
                            fw.op(A, lambda: na.activation(out=ob[0:32, 0:m], in_=p1[0:32, 0:m], func=AF.Copy), [p1.b], [ob.b])
                            fw.dma(SP, Blr.ap[:, t0 + o:t0 + o + m], ob[0:32, 0:m], [ob.b], Blr.B(0, 32, t0 + o, t0 + o + m))
                    else:
                        (c0, ncols, dst) = seg[1]
                        for tb in range(n // 128):
                            p1 = PS.next()
                            for kc in range(8):
                                fw.op(PE, lambda: nt.matmul(p1[:, 0:ncols], lhsT=h[:, kc, tb * 128:(tb + 1) * 128], rhs=wt[:, kc, 0:ncols],
                                                            start=(kc == 0), stop=(kc == 7)), [wt.b, h.b], [p1.b], self_sync=False)
                            ob = ob_r.next()
                            flip[0] ^= 1
                            if flip[0]:
                                fw.op(A, lambda: na.activation(out=ob[:, 0:ncols], in_=p1[:, 0:ncols], func=AF.Copy), [p1.b], [ob.b])
                            else:
                                fw.op(V, lambda: nv.tensor_copy(out=ob[:, 0:ncols], in_=p1[:, 0:ncols]), [p1.b], [ob.b])
                            r0 = t0 + tb * 128
                            fw.dma(SP, dst.ap[r0:r0 + 128, 0:ncols], ob[:, 0:ncols], [ob.b], dst.B(r0, r0 + 128, 0, ncols))
                if h_next is not None:
                    h = h_next

    def mixA_phase(l, with_ctx, after_head1_loads=None):
        with fw.phase():
            K_r = fw.ring("Ka", [128, NT], BF16, 2)
            Q_r = fw.ring("Qa", [128, NT], BF16, 2)
            V_r = fw.ring("Va", [128, 34, 128], BF16, 2)
            p_r = fw.ring("Pa", [128, 512], BF16, 6)
            ev_r = fw.ring("eva", [128, 4, 512], F32, 2)
            f_r = fw.ring("fa", [128, 512], F32, 2)
            sq_r = fw.ring("sqa", [128, 512], BF16, 2)
            ob_r = fw.ring("oba", [128, 512], BF16, 2)
            Oacc = [PSB[0], PSB[2]]
            Dacc = [PSB[1], PSB[3]]
            SR = Ring(PSB[4:8])
            pending = []
            DEFER = 8
            dacc_r = fw.ring("dacc", [128, 512], F32, 2)
            dcur = [None]
            def ld_head(hd):
                Kt = K_r.next(); Qt = Q_r.next(); Vt = V_r.next()
                fw.dma(SP, Kt[:], Ak.ap[hd * 128:(hd + 1) * 128, :], Ak.B(hd * 128, (hd + 1) * 128, 0, NT), [Kt.b])
                fw.dma(SP, Qt[:], Aq.ap[hd * 128:(hd + 1) * 128, :], Aq.B(hd * 128, (hd + 1) * 128, 0, NT), [Qt.b])
                fw.dma(SP, Vt[:], Av.ap[:, hd * 128:(hd + 1) * 128].rearrange("(c p) v -> p c v", p=128),
                       Av.B(0, NT, hd * 128, (hd + 1) * 128), [Vt.b])
                return (Kt, Qt, Vt)

            nxt_head = ld_head(0)
            for hd in range(4):
                (Kt, Qt, Vt) = nxt_head
                if hd + 1 < 4:
                    nxt_head = ld_head(hd + 1)
                if hd == 0 and after_head1_loads is not None:
                    after_head1_loads([Kt.b, Qt.b, Vt.b, nxt_head[0].b, nxt_head[1].b, nxt_head[2].b])
                qtiles = [(q0, 512, list(range(34))) for q0 in range(0, SEQ, 512)]
                if with_ctx:
                    qtiles.append((SEQ, 256, [32, 33]))
                for (q0, m, kcs) in qtiles:
                    S_t = {}

                    def emit_qk(i):
                        kc = kcs[i]
                        for mm in range(2):
                            S = SR.next()
                            fw.op(PE, lambda: nt.matmul(S[:, 0:m], lhsT=Kt[mm * 64:(mm + 1) * 64, kc * 128:(kc + 1) * 128],
                                                        rhs=Qt[mm * 64:(mm + 1) * 64, q0:q0 + m], start=True, stop=True),
                                  [Kt.b, Qt.b], [S.b], self_sync=False)
                            S_t[(i, mm)] = S

                    def emit_pv(i):
                        kc = kcs[i]
                        first = (i == 0); lastk = (i == len(kcs) - 1)
                        for mm in range(2):
                            S = S_t.pop((i, mm))
                            P = p_r.next()
                            fw.op(A, lambda: na.activation(out=P[:, 0:m], in_=S[:, 0:m], func=AF.Exp, scale=0.125), [S.b], [P.b])
                            if mm == 0:
                                if first:
                                    dcur[0] = dacc_r.next()
                                    fw.op(V, lambda: nv.tensor_copy(out=dcur[0][:, 0:m], in_=P[:, 0:m]), [P.b], [dcur[0].b])
                                else:
                                    fw.op(V, lambda: nv.tensor_tensor(out=dcur[0][:, 0:m], in0=dcur[0][:, 0:m], in1=P[:, 0:m], op=ALU.add),
                                          [P.b, dcur[0].b], [dcur[0].b])
                            else:
                                fw.op(PE, lambda: nt.matmul(Dacc[mm][:, 0:m], lhsT=ones_bf[:], rhs=P[:, 0:m], start=first, stop=lastk),
                                      [P.b, ones_bf.b], [Dacc[mm].b], self_sync=False)
                            fw.op(PE, lambda: nt.matmul(Oacc[mm][:, 0:m], lhsT=Vt[:, kc, :], rhs=P[:, 0:m], start=first, stop=lastk),
                                  [P.b, Vt.b], [Oacc[mm].b], self_sync=False)
                    n_st = len(kcs)
                    pv_done = set()
                    for i in range(n_st + 1):
                        if i < n_st:
                            emit_qk(i)
                        if i >= 1 and (i - 1) not in pv_done:
                            emit_pv(i - 1)
                        if (i == DEFER or i == n_st) and pending:
                            if i < n_st:
                                emit_pv(i)
                                pv_done.add(i)
                            while pending:
                                pending.pop(0)()
                    fw.op(PE, lambda: nt.matmul(Dacc[0][:, 0:m], lhsT=ones_ff[:], rhs=dcur[0][:, 0:m], start=True, stop=True),
                          [dcur[0].b, ones_ff.b], [Dacc[0].b], self_sync=False)
                    ev = ev_r.next()
                    for mm_ in range(2):
                        fw.op(A, lambda: na.activation(out=ev[:, 1 + 2 * mm_, 0:m], in_=Dacc[mm_][:, 0:m], func=AF.Ln), [Dacc[mm_].b], [ev.b])
                        fw.op(A, lambda: na.activation(out=ev[:, 1 + 2 * mm_, 0:m], in_=ev[:, 1 + 2 * mm_, 0:m], func=AF.Exp, scale=-1.0), [ev.b], [ev.b])
                    fw.op(V, lambda: nv.tensor_tensor(out=ev[:, 0, 0:m], in0=Oacc[0][:, 0:m], in1=ev[:, 1, 0:m], op=ALU.mult), [Oacc[0].b, ev.b], [ev.b])
                    fw.op(V, lambda: nv.tensor_tensor(out=ev[:, 2, 0:m], in0=Oacc[1][:, 0:m], in1=ev[:, 3, 0:m], op=ALU.mult), [Oacc[1].b, ev.b], [ev.b])
                    fw.op(V, lambda: nv.scalar_tensor_tensor(out=ev[:, 0, 0:m], in0=ev[:, 2, 0:m], scalar=neglam[:, l:l + 1], in1=ev[:, 0, 0:m],
                                                             op0=ALU.mult, op1=ALU.add), [ev.b, neglam.b], [ev.b])
                    sq = sq_r.next()
                    fw.op(G, lambda: ng.tensor_tensor(out=sq[:, 0:m], in0=ev[:, 0, 0:m], in1=ev[:, 0, 0:m], op=ALU.mult), [ev.b], [sq.b])

                    def part2(ev=ev, sq=sq, m=m, q0=q0, hd=hd):
                        ss = SR.next()
                        fw.op(PE, lambda: nt.matmul(ss[:, 0:m], lhsT=ones_bf[:], rhs=sq[:, 0:m], start=True, stop=True),
                              [sq.b, ones_bf.b], [ss.b], self_sync=False)
                        rs = f_r.next()
                        fw.op(A, lambda: na.activation(out=rs[:, 0:m], in_=ss[:, 0:m], func=AF.Ln, scale=1.0 / 128, bias=eps_t[:, 0:1]), [ss.b, eps_t.b], [rs.b])
                        fw.op(A, lambda: na.activation(out=rs[:, 0:m], in_=rs[:, 0:m], func=AF.Exp, scale=-0.5), [rs.b], [rs.b])
                        ob = ob_r.next()
                        fw.op(V, lambda: nv.scalar_tensor_tensor(out=ob[:, 0:m], in0=ev[:, 0, 0:m], scalar=subl[:, l:l + 1], in1=rs[:, 0:m],
                                                                 op0=ALU.mult, op1=ALU.mult), [ev.b, rs.b, subl.b], [ob.b])
                        fw.dma(SP, Oa.ap[hd * 128:(hd + 1) * 128, q0:q0 + m], ob[:, 0:m], [ob.b], Oa.B(hd * 128, (hd + 1) * 128, q0, q0 + m))
                    pending.append(part2)
            while pending:
                pending.pop(0)()

    def mixC_phase(l, with_ctx):
        with fw.phase():
            Qg_r = fw.ring("Qg", [64, 4, NT], BF16, 2)
            Kc_r = fw.ring("Kc", [64, NT], BF16, 2)
            Vc_r = fw.ring("Vc", [128, 34, 64], BF16, 2)
            esr = fw.sbuf("esr", [64, 2, 512], F32)
            p_r = fw.ring("Pc", [128, 512], BF16, 6)
            f_r = fw.ring("fc", [64, 512], F32, 4)
            ob_r = fw.ring("obc", [64, 512], BF16, 3)
            Oacc = Ring([PSB[0], PSB[2]])
            Dacc = Ring([PSB[1], PSB[3]])
            SR = Ring(PSB[4:8])
            for hk in range(2):
                for hh in range(4):
                    fw.op(V, lambda: nv.tensor_copy(out=esr[:, hk, hh * 128:(hh + 1) * 128],
                                                    in_=esink[0:64, l, hk * 4 + hh:hk * 4 + hh + 1].to_broadcast([64, 128])),
                          [esink.b], [esr.b])
            def ld_group(hk):
                Qg = Qg_r.next(); Kc = Kc_r.next(); Vc = Vc_r.next()
                for hh in range(4):
                    r0 = (hk * 4 + hh) * 64
                    fw.dma(SP, Qg[:, hh, :], Cq.ap[r0:r0 + 64, :], Cq.B(r0, r0 + 64, 0, NT), [Qg.b])
                fw.dma(SP, Kc[:], Ck.ap[hk * 64:(hk + 1) * 64, :], Ck.B(hk * 64, (hk + 1) * 64, 0, NT), [Kc.b])
                fw.dma(SP, Vc[:], Cv.ap[:, hk * 64:(hk + 1) * 64].rearrange("(c p) v -> p c v", p=128),
                       Cv.B(0, NT, hk * 64, (hk + 1) * 64), [Vc.b])
                return (Qg, Kc, Vc)

            groups_c = [ld_group(0), ld_group(1)]
            for hk in range(2):
                (Qg, Kc, Vc) = groups_c[hk]
                blocks = list(range(32)) + ([32, 33] if with_ctx else [])
                flat = []
                for b in blocks:
                    if b < 32:
                        chunks = ([(b - 1, 1)] if b > 0 else []) + [(b, None)] + ([(b + 1, 0)] if b < 31 else []) + [(32, None), (33, None)]
                    else:
                        chunks = [(32, None), (33, None)]
                    for ci, (kc, mk) in enumerate(chunks):
                        flat.append((b, kc, mk, ci == 0, ci == len(chunks) - 1))
                S_t = {}
                acc = {}

                def emit_qk(i):
                    b, kc, mk, first, lastk = flat[i]
                    S = SR.next()
                    fw.op(PE, lambda: nt.matmul(S[:, :].rearrange("p (h t) -> p h t", h=4), lhsT=Kc[:, kc * 128:(kc + 1) * 128],
                                                rhs=Qg[:, :, b * 128:(b + 1) * 128], start=True, stop=True),
                          [Kc.b, Qg.b], [S.b], self_sync=False)
                    S_t[i] = S

                def emit_pv(i):
                    b, kc, mk, first, lastk = flat[i]
                    S = S_t.pop(i)
                    if first:
                        acc[b] = (Oacc.next(), Dacc.next())
                    Oa_, Da_ = acc[b]
                    P = p_r.next()
                    fw.op(A, lambda: na.activation(out=P[:], in_=S[:], func=AF.Exp, scale=0.125), [S.b], [P.b])
                    if mk is not None:
                        for hh in range(4):
                            fw.op(V, lambda: nv.tensor_tensor(out=P[:, hh * 128:(hh + 1) * 128], in0=P[:, hh * 128:(hh + 1) * 128],
                                                              in1=tri_b[:, mk, :], op=ALU.mult), [P.b, tri_b.b], [P.b])
                    fw.op(PE, lambda: nt.matmul(Da_[0:64, :], lhsT=ones_bf[:, 0:64], rhs=P[:], start=first, stop=lastk),
                          [P.b, ones_bf.b], [Da_.b], self_sync=False)
                    fw.op(PE, lambda: nt.matmul(Oa_[0:64, :], lhsT=Vc[:, kc, :], rhs=P[:], start=first, stop=lastk),
                          [P.b, Vc.b], [Oa_.b], self_sync=False)
                    if lastk:
                        del acc[b]
                        dt_ = f_r.next(); ob = ob_r.next()
                        fw.op(V, lambda: nv.tensor_tensor(out=dt_[:], in0=Da_[0:64, :], in1=esr[:, hk, :], op=ALU.add), [Da_.b, esr.b], [dt_.b])
                        fw.op(A, lambda: na.activation(out=dt_[:], in_=dt_[:], func=AF.Ln), [dt_.b], [dt_.b])
                        fw.op(A, lambda: na.activation(out=dt_[:], in_=dt_[:], func=AF.Exp, scale=-1.0), [dt_.b], [dt_.b])
                        fw.op(V, lambda: nv.tensor_tensor(out=ob[:], in0=Oa_[0:64, :], in1=dt_[:], op=ALU.mult), [Oa_.b, dt_.b], [ob.b])
                        fw.dma(SP, Oc.ap[hk * 256:(hk + 1) * 256, b * 128:(b + 1) * 128].rearrange("(h d) t -> d h t", d=64),
                               ob[:, :].rearrange("d (h t) -> d h t", h=4), [ob.b], Oc.B(hk * 256, (hk + 1) * 256, b * 128, (b + 1) * 128))
                SK = 3
                for i in range(len(flat) + SK):
                    if i < len(flat):
                        emit_qk(i)
                    if i >= SK:
                        emit_pv(i - SK)

    def mixB_phase(l, with_ctx):
        with fw.phase():
            SBs = [[fw.sbuf("SB%d%d" % (d, pr_), [128, 34, 128], BF16) for pr_ in range(2)] for d in range(2)]
            S32 = [[fw.sbuf("S32%d%d" % (d, pr_), [128, 128], F32) for pr_ in range(2)] for d in range(2)]
            gw = fw.sbuf("gw", [64, 512], BF16)
            lra = fw.sbuf("lra", [64, NT], BF16)
            fw.dma(G, gw[:], gw2_in[l], [], [gw.b])
            fw.op(V, lambda: nv.memset(lra[32:64, :], 1.0), [], [lra.b])
            fw.dma(SP, lra[0:32, :], Blr.ap[:, :], Blr.B(0, 32, 0, NT), [lra.b])
            order = {0: [32, 33] + list(range(32)), 1: [33, 32] + list(range(31, -1, -1))}
            for d in range(2):
                for pr_ in range(2):
                    fw.op(V, lambda: nv.memset(S32[d][pr_][:], 0.0), [], [S32[d][pr_].b])
            def pipeline(gens):
                gens = list(gens)
                active = []
                gi = 0
                while gi < len(gens) or active:
                    if gi < len(gens):
                        active.append(gens[gi]); gi += 1
                    nxt = []
                    for g_ in active:
                        try:
                            next(g_); nxt.append(g_)
                        except StopIteration:
                            pass
                    active = nxt

            kt_r = fw.ring("ktb", [128, 256], BF16, 3)
            vt_r = fw.ring("vtb", [128, 512], BF16, 5)
            sp1_r = fw.ring("sp1", [128, 256], F32, 4)
            e3_r = fw.ring("e3b", [128, 256], F32, 3)
            kd_r = fw.ring("kdb", [128, 256], BF16, 4)
            dec_r = fw.ring("decb", [128, 4], F32, 4)

            def p1_gen(i, d):
                c = order[d][i]
                z = PS.next()
                fw.op(PE, lambda: nt.matmul(z[:, 0:256], lhsT=lra[:, c * 128:(c + 1) * 128], rhs=gw[:, d * 256:(d + 1) * 256],
                                            start=True, stop=True), [lra.b, gw.b], [z.b], self_sync=False)
                sp = sp1_r.next()
                fw.op(A, lambda: na.activation(out=sp[:], in_=z[:, 0:256], func=AF.Exp, scale=-1.0), [z.b], [sp.b])
                fw.op(A, lambda: na.activation(out=sp[:], in_=sp[:], func=AF.Ln, bias=1.0), [sp.b], [sp.b])
                yield
                kt = kt_r.next(); vt = vt_r.next()
                fw.dma(SP, kt[:], BkT.ap[c * 128:(c + 1) * 128, :], BkT.B(c * 128, (c + 1) * 128, 0, 256), [kt.b])
                fw.dma(SP, vt[:], Bv.ap[c * 128:(c + 1) * 128, :], Bv.B(c * 128, (c + 1) * 128, 0, 512), [vt.b])
                rem = PS.next()
                fw.op(PE, lambda: nt.matmul(rem[:, 0:256], lhsT=tri_f[:, 2 + d, :], rhs=sp[:], start=True, stop=True),
                      [tri_f.b, sp.b], [rem.b], self_sync=False)
                for pr_ in range(2):
                    fw.op(PE, lambda: nt.matmul(rem[:, 256 + 2 * pr_:258 + 2 * pr_], lhsT=sp[:, pr_ * 128:(pr_ + 1) * 128], rhs=ones_f[:, 0:2],
                                                start=True, stop=True), [sp.b, ones_f.b], [rem.b], self_sync=False)
                e3 = e3_r.next()
                fw.op(A, lambda: na.activation(out=e3[:], in_=rem[:, 0:256], func=AF.Exp, scale=-1.0 / 16), [rem.b], [e3.b])
                dec = dec_r.next()
                fw.op(A, lambda: na.activation(out=dec[:], in_=rem[:, 256:260], func=AF.Exp, scale=-1.0 / 16), [rem.b], [dec.b])
                kd = kd_r.next()
                fw.op(V, lambda: nv.tensor_tensor(out=kd[:], in0=kt[:], in1=e3[:], op=ALU.mult), [kt.b, e3.b], [kd.b])
                yield
                for pr_ in range(2):
                    kv = PS.next()
                    fw.op(PE, lambda: nt.matmul(kv[:, 0:256], lhsT=kd[:, pr_ * 128:(pr_ + 1) * 128], rhs=vt[:, pr_ * 256:(pr_ + 1) * 256],
                                                start=True, stop=True), [kd.b, vt.b], [kv.b], self_sync=False)
                    st = S32[d][pr_]
                    fw.op(G, lambda: ng.tensor_copy(out=SBs[d][pr_][:, c, :], in_=st[:]), [st.b], [SBs[d][pr_].b])
                    for hh in range(2):
                        fw.op(V, lambda: nv.scalar_tensor_tensor(
                            out=st[hh * 64:(hh + 1) * 64, :], in0=st[hh * 64:(hh + 1) * 64, :],
                            scalar=dec[hh * 64:(hh + 1) * 64, 2 * pr_:2 * pr_ + 1],
                            in1=kv[hh * 64:(hh + 1) * 64, hh * 128:(hh + 1) * 128], op0=ALU.mult, op1=ALU.add),
                            [st.b, dec.b, kv.b], [st.b])

            if "x" not in mixers:
                pipeline(p1_gen(i, d) for i in range(34) for d in range(2))

            sp_r = fw.ring("spb", [128, 512], F32, 3)
            e_r = fw.ring("eb", [128, 512], F32, 4)
            q_r = fw.ring("qtb", [128, 2, 128], BF16, 3)
            k_r = fw.ring("k2b", [128, 2, 128], BF16, 3)
            qe_r = fw.ring("qeb", [128, 2, 4, 128], BF16, 5)
            ke_r = fw.ring("keb", [128, 2, 4, 128], BF16, 4)
            for rg in (qe_r, ke_r):
                for tq in rg.items:
                    fw.op(V, lambda: nv.memset(tq[:], 0.0), [], [tq.b])
            at_r = fw.ring("atb", [128, 4, 128], BF16, 6)
            osb_r = fw.ring("osb", [128, 512], F32, 3)
            osq_r = fw.ring("osq", [128, 512], BF16, 3)
            rs_r = fw.ring("rsb", [128, 512], F32, 2)
            tt_r = fw.ring("ttb", [128, 512], F32, 2)
            rt_r = fw.ring("rtb", [128, 4, 128], BF16, 2)
            ob_r = fw.ring("obb", [128, 4, 128], BF16, 2)
            vt2_r = fw.ring("vt2", [128, 512], BF16, 3)

            def p2_gen(c):
                cs = slice(c * 128, (c + 1) * 128)
                z = PS.next()
                fw.op(PE, lambda: nt.matmul(z[:], lhsT=lra[:, cs], rhs=gw[:, :], start=True, stop=True), [lra.b, gw.b], [z.b], self_sync=False)
                sp = sp_r.next()
                fw.op(A, lambda: na.activation(out=sp[:], in_=z[:], func=AF.Exp, scale=-1.0), [z.b], [sp.b])
                fw.op(A, lambda: na.activation(out=sp[:], in_=sp[:], func=AF.Ln, bias=1.0), [sp.b], [sp.b])
                qT = q_r.next(); kT = k_r.next()
                fw.dma(SP, qT[:], Bq.ap[:, cs].rearrange("(r p) t -> p r t", p=128), Bq.B(0, 256, c * 128, (c + 1) * 128), [qT.b])
                fw.dma(SP, kT[:], Bk.ap[:, cs].rearrange("(r p) t -> p r t", p=128), Bk.B(0, 256, c * 128, (c + 1) * 128), [kT.b])
                yield
                cm = PS.next()
                for d in range(2):
                    for pr_ in range(2):
                        q = d * 2 + pr_
                        fw.op(PE, lambda: nt.matmul(cm[:, q * 128:(q + 1) * 128], lhsT=sp[:, d * 256 + pr_ * 128:d * 256 + (pr_ + 1) * 128],
                                                    rhs=tri_f[:, d, :], start=True, stop=True), [sp.b, tri_f.b], [cm.b], self_sync=False)
                E1 = e_r.next(); E2 = e_r.next()
                fw.op(A, lambda: na.activation(out=E1[:], in_=cm[:], func=AF.Exp, scale=-1.0 / 16), [cm.b], [E1.b])
                fw.op(A, lambda: na.activation(out=E2[:], in_=cm[:], func=AF.Exp, scale=1.0 / 16), [cm.b], [E2.b])
                qe = qe_r.next(); ke = ke_r.next()
                for d in range(2):
                    for hh in range(2):
                        pp = slice(hh * 64, (hh + 1) * 64)
                        fw.op(V, lambda: nv.scalar_tensor_tensor(out=qe[pp, hh, 2 * d:2 * d + 2, :], in0=qT[pp], scalar=0.125,
                                                                 in1=E1[pp, d * 256:(d + 1) * 256].rearrange("p (r t) -> p r t", r=2),
                                                                 op0=ALU.mult, op1=ALU.mult), [qT.b, E1.b], [qe.b])
                        fw.op(V, lambda: nv.tensor_tensor(out=ke[pp, hh, 2 * d:2 * d + 2, :], in0=kT[pp],
                                                          in1=E2[pp, d * 256:(d + 1) * 256].rearrange("p (r t) -> p r t", r=2), op=ALU.mult),
                              [kT.b, E2.b], [ke.b])
                yield
                atts = []
                for d in range(2):
                    at = PS.next()
                    for hd in range(4):
                        pr_, hh = divmod(hd, 2)
                        fw.op(PE, lambda: nt.matmul(at[:, hd * 128:(hd + 1) * 128], lhsT=ke[:, hh, 2 * d + pr_, :],
                                                    rhs=qe[:, hh, 2 * d + pr_, :], start=True, stop=True),
                              [ke.b, qe.b], [at.b], self_sync=False)
                    att = at_r.next()
                    for hd in range(4):
                        fw.op(V, lambda: nv.tensor_tensor(out=att[:, hd, :], in0=at[:, hd * 128:(hd + 1) * 128],
                                                          in1=tri_f[:, d, :], op=ALU.mult),
                              [at.b, tri_f.b], [att.b])
                    atts.append(att)
                vt = vt2_r.next()
                fw.dma(SP, vt[:], Bv.ap[cs, :], Bv.B(c * 128, (c + 1) * 128, 0, 512), [vt.b])
                yield
                po = PS.next()
                for hd in range(4):
                    pr_, hh = divmod(hd, 2)
                    hs = slice(hd * 128, (hd + 1) * 128)
                    seq = [(vt[:, hs], atts[0][:, hd, :], [vt.b, atts[0].b]),
                           (SBs[0][pr_][:, c, :], qe[:, hh, pr_, :], [SBs[0][pr_].b, qe.b]),
                           (vt[:, hs], atts[1][:, hd, :], [vt.b, atts[1].b]),
                           (SBs[1][pr_][:, c, :], qe[:, hh, 2 + pr_, :], [SBs[1][pr_].b, qe.b])]
                    for qi, (lh, rh, rd) in enumerate(seq):
                        fw.op(PE, lambda: nt.matmul(po[:, hs], lhsT=lh, rhs=rh, start=(qi == 0), stop=(qi == 3)), rd, [po.b], self_sync=False)
                osb = osb_r.next(); osq = osq_r.next()
                fw.op(A, lambda: na.activation(out=osb[:], in_=po[:], func=AF.Copy), [po.b], [osb.b])
                fw.op(G, lambda: ng.tensor_tensor(out=osq[:], in0=osb[:], in1=osb[:], op=ALU.mult), [osb.b], [osq.b])
                yield
                rs = rs_r.next(); tt = tt_r.next()
                ss = PS.next()
                fw.op(PE, lambda: nt.matmul(ss[:], lhsT=ones_bf[:], rhs=osq[:], start=True, stop=True), [osq.b, ones_bf.b], [ss.b], self_sync=False)
                fw.op(A, lambda: na.activation(out=rs[:], in_=ss[:], func=AF.Ln, scale=1.0 / 128, bias=eps_t[:, 0:1]), [ss.b, eps_t.b], [rs.b])
                fw.op(A, lambda: na.activation(out=rs[:], in_=rs[:], func=AF.Exp, scale=-0.5), [rs.b], [rs.b])
                fw.op(V, lambda: nv.scalar_tensor_tensor(out=tt[:], in0=osb[:], scalar=gn[:, l:l + 1], in1=rs[:], op0=ALU.mult, op1=ALU.mult),
                      [osb.b, rs.b, gn.b], [tt.b])
                rT = rt_r.next()
                fw.dma(SP, rT[:], Br.ap[:, cs].rearrange("(h p) t -> p h t", p=128), Br.B(0, 512, c * 128, (c + 1) * 128), [rT.b])
                ob = ob_r.next()
                fw.op(G, lambda: ng.tensor_tensor(out=ob[:], in0=tt[:, :].rearrange("p (h t) -> p h t", h=4), in1=rT[:], op=ALU.mult),
                      [tt.b, rT.b], [ob.b])
                fw.dma(SP, Ob.ap[:, cs].rearrange("(h p) t -> p h t", p=128), ob[:], [ob.b], Ob.B(0, 512, c * 128, (c + 1) * 128))

            if "y" not in mixers:
                pipeline(p2_gen(c) for c in (range(34) if with_ctx else range(32)))

    def merge_phase(l, with_ctx):
        Wm = WB[l]["mix"]
        with fw.phase():
            h2 = fw.sbuf("h2m", [128, 8, 1024], BF16)
            oa = fw.sbuf("oam", [128, 4, 1024], BF16)
            obm = fw.sbuf("obm", [128, 4, 1024], BF16)
            ocm = fw.sbuf("ocm", [64, 8, 1024], BF16)
            y = fw.sbuf("ym", [128, 8, 1024], BF16)
            Wa = fw.sbuf("Wa", [128, 4, D], BF16)
            Wb = fw.sbuf("Wb", [128, 4, D], BF16)
            Wc = fw.sbuf("Wc", [64, 8, D], BF16)
            Wo = fw.sbuf("Wo", [128, 8, D], BF16)
            wg_r = fw.ring("wg", [128, 8, 3, 128], BF16, 2)
            sg_r = fw.ring("sgm", [128, 512], F32, 3)
            y_r = fw.ring("yr", [128, 512], F32, 4)
            xr_r = fw.ring("xr", [128, 512], F32, 3)
            xo_r = fw.ring("xo", [128, 512], F32, 3)
            ld_w(SP, Wa[:], WB[l]["br"][0], 0, 4, 0, D, Wa)
            ld_w(SP, Wb[:], WB[l]["br"][1], 0, 4, 0, D, Wb)
            apc, bc = WB[l]["br"][2]
            fw.dma(SP, Wc[:], apc.rearrange("(h d) c -> d h c", d=64), [bc], [Wc.b])
            ld_w(SP, Wo[:], WB[l]["out"], 0, 8, 0, D, Wo)
            sts = [(t0, n) for (t0, n) in STS if not (t0 >= SEQ and not with_ctx)]

            def ld_inputs(t0, n):
                fw.dma(SP, h2[:, :, 0:n], h2s.ap[:, t0:t0 + n].rearrange("(k p) n -> p k n", p=128), h2s.B(0, D, t0, t0 + n), [h2.b])
                fw.dma(SP, oa[:, :, 0:n], Oa.ap[:, t0:t0 + n].rearrange("(k p) n -> p k n", p=128), Oa.B(0, 512, t0, t0 + n), [oa.b])
                fw.dma(SP, obm[:, :, 0:n], Ob.ap[:, t0:t0 + n].rearrange("(k p) n -> p k n", p=128), Ob.B(0, 512, t0, t0 + n), [obm.b])
                fw.dma(SP, ocm[:, :, 0:n], Oc.ap[:, t0:t0 + n].rearrange("(h d) n -> d h n", d=64), Oc.B(0, 512, t0, t0 + n), [ocm.b])

            def ld_wg(oc):
                wg = wg_r.next()
                for br in range(3):
                    ld_w(SP, wg[:, :, br, :], Wm, 0, 8, O_G + br * D + oc * 128, 128, wg)
                return wg

            ld_inputs(*sts[0])
            nxt_wg = ld_wg(0)
            for si, (t0, n) in enumerate(sts):
                isctx = t0 >= SEQ
                tls = tiles_of(0, n)
                for oc in range(8):
                    wg = nxt_wg
                    if oc + 1 < 8:
                        nxt_wg = ld_wg(oc + 1)
                    elif si + 1 < len(sts):
                        nxt_wg = ld_wg(0)
                    for (o, m) in tls:
                        pgs = []
                        for br in range(3):
                            pg = PS.next()
                            for kc in range(8):
                                fw.op(PE, lambda: nt.matmul(pg[:, 0:m], lhsT=wg[:, kc, br, :], rhs=h2[:, kc, o:o + m],
                                                            start=(kc == 0), stop=(kc == 7)), [wg.b, h2.b], [pg.b], self_sync=False)
                            pgs.append(pg)
                        ys = []
                        for br in range(3):
                            pb = PS.next()
                            if br < 2:
                                Wt, src = (Wa, oa) if br == 0 else (Wb, obm)
                                for kc in range(4):
                                    fw.op(PE, lambda: nt.matmul(pb[:, 0:m], lhsT=Wt[:, kc, oc * 128:(oc + 1) * 128], rhs=src[:, kc, o:o + m],
                                                                start=(kc == 0), stop=(kc == 3)), [Wt.b, src.b], [pb.b], self_sync=False)
                            else:
                                for hd in range(8):
                                    fw.op(PE, lambda: nt.matmul(pb[:, 0:m], lhsT=Wc[:, hd, oc * 128:(oc + 1) * 128], rhs=ocm[:, hd, o:o + m],
                                                                start=(hd == 0), stop=(hd == 7)), [Wc.b, ocm.b], [pb.b], self_sync=False)
                            sg_ = sg_r.next()
                            fw.op(A, lambda: na.activation(out=sg_[:, 0:m], in_=pgs[br][:, 0:m], func=AF.Sigmoid), [pgs[br].b], [sg_.b])
                            yt = y_r.next()
                            fw.op(V, lambda: nv.tensor_tensor(out=yt[:, 0:m], in0=sg_[:, 0:m], in1=pb[:, 0:m], op=ALU.mult), [sg_.b, pb.b], [yt.b])
                            ys.append(yt)
                        y12 = y_r.next()
                        fw.op(G, lambda: ng.tensor_tensor(out=y12[:, 0:m], in0=ys[0][:, 0:m], in1=ys[1][:, 0:m], op=ALU.add), [ys[0].b, ys[1].b], [y12.b])
                        fw.op(G, lambda: ng.tensor_tensor(out=y[:, oc, o:o + m], in0=y12[:, 0:m], in1=ys[2][:, 0:m], op=ALU.add), [y12.b, ys[2].b], [y.b])
                if si + 1 < len(sts):
                    ld_inputs(*sts[si + 1])
                for oc in range(8):
                    for (o, m) in tls:
                        pt = PS.next()
                        for kc in range(8):
                            fw.op(PE, lambda: nt.matmul(pt[:, 0:m], lhsT=Wo[:, kc, oc * 128:(oc + 1) * 128], rhs=y[:, kc, o:o + m],
                                                        start=(kc == 0), stop=(kc == 7)), [Wo.b, y.b], [pt.b], self_sync=False)
                        residual_out(l, 1, pt, oc, t0 + o, m, isctx, xr_r, xo_r)

    def final_phase():
        with fw.phase():
            rings = (fw.ring("xt", [128, 8, 256], F32, 2), fw.ring("sq", [128, 256], BF16, 3),
                     fw.ring("rs", [128, 256], F32, 2), fw.ring("tm", [128, 256], F32, 3))
            fo_r = fw.ring("fo", [128, 8, 256], F32, 2)
            for t0 in range(0, SEQ, 256):
                xt, rs = norm_mod(0, 0, xs, t0, 256, False, None, 0, rings)
                fo = fo_r.next()
                for kc in range(8):
                    fw.op(V, lambda: nv.scalar_tensor_tensor(out=fo[:, kc, :], in0=xt[:, kc, :], scalar=fg[:, kc:kc + 1], in1=rs[:, 0:256],
                                                             op0=ALU.mult, op1=ALU.mult), [xt.b, rs.b, fg.b], [fo.b])
                fw.dma(SP, out_dt.ap[:, t0:t0 + 256].rearrange("(k p) n -> p k n", p=128), fo[:], [fo.b], out_dt.B(0, D, t0, t0 + 256))

    for l in range(n_layers):
        last = (l == L_ALL - 1)
        with_ctx = not last
        ffn_phase(l, 0)
        if stop_after == ("ffn1", l):
            break
        mixin_phase(l)
        if stop_after == ("mixin", l):
            break
        def _casts(after=(), l=l):
            if l + 1 < n_layers:
                convert_layer(l + 1, 0, after)
                convert_layer(l + 1, 1, after)
        if "A" in mixers:
            mixA_phase(l, with_ctx, _casts)
        else:
            _casts()
        if "C" in mixers:
            mixC_phase(l, with_ctx)
        if "B" in mixers:
            mixB_phase(l, with_ctx)
        if stop_after == ("mixers", l):
            break
        merge_phase(l, with_ctx)
        if stop_after == ("mix", l):
            break
        ffn_phase(l, 1, skip_ctx=last)
        if stop_after == ("ffn2", l):
            break
    else:
        if stop_after is None:
            final_phase()

    if isinstance(dump, (list, tuple)):
        dts = dict(xs=xs, h2s=h2s, Aq=Aq, Ak=Ak, Av=Av, Bq=Bq, Bk=Bk, BkT=BkT, Bv=Bv, Blr=Blr, Br=Br, Cq=Cq, Ck=Ck, Cv=Cv, Oa=Oa, Ob=Ob, Oc=Oc)
        for nm in dump:
            src = dts[nm]
            shp = list(src.ap.shape)
            dd = nc.dram_tensor("dbg_" + nm, shp, F32, kind="ExternalOutput").ap()
            fw.dma(G, dd, src.ap, src.B(0, shp[0], 0, shp[1]), [Buf("dbg")])
    if dump == "mod":
        ob = out_dt.B(0, D, 0, SEQ)
        fw.dma(SP, out_dt.ap[0:128, 0:144], modT[:, 0].rearrange("p i c w -> p (i c w)"), [modT.b], ob)
        fw.dma(SP, out_dt.ap[0:128, 144:192], gsT[:, 0].rearrange("p i c w -> p (i c w)"), [gsT.b], ob)
        fw.dma(SP, out_dt.ap[0:128, 192:240], cfT[:, 0].rearrange("p i c w -> p (i c w)"), [cfT.b], ob)
    if dump == "xs":
        with fw.phase():
            cp_r = fw.ring("cp", [128, 8, 256], F32, 2)
            for t0 in range(0, SEQ, 256):
                cp = cp_r.next()
                fw.dma(SP, cp[:], xs.ap[:, t0:t0 + 256].rearrange("(k p) n -> p k n", p=128), xs.B(0, D, t0, t0 + 256), [cp.b])
                fw.dma(SP, out_dt.ap[:, t0:t0 + 256].rearrange("(k p) n -> p k n", p=128), cp[:], [cp.b], out_dt.B(0, D, t0, t0 + 256))
    fw.barrier()
    fw.close()
    return nc


def _rope_tables_np():
    rows = SEQ // 64
    row = np.repeat(np.arange(rows, dtype=np.float32), 64)
    col = np.tile(np.arange(64, dtype=np.float32), rows)
    nf = 16
    freqs = np.power(np.float32(10000.0), -np.arange(nf, dtype=np.float32) / nf).astype(np.float32)
    ar = row[:, None] * freqs
    ac = col[:, None] * freqs
    ang = np.concatenate([ar, ar, ac, ac], axis=-1).astype(np.float32)
    cos = np.cos(ang).astype(np.float32)
    sin = np.sin(ang).astype(np.float32)
    sign = np.ones(64, np.float32)
    for a in range(2):
        sign[a * 32:a * 32 + 16] = -1.0
    sinS = sin * sign[None, :]
    C = np.ones((64, NT), np.float32)
    S = np.zeros((64, NT), np.float32)
    C[:, :SEQ] = cos.T
    S[:, :SEQ] = sinS.T
    return np.ascontiguousarray(np.concatenate([C, C], 0)), np.ascontiguousarray(np.concatenate([S, S], 0))


def _perm64():
    p = np.zeros(64, np.int64)
    for a in range(2):
        for i in range(16):
            p[a * 32 + i] = a * 32 + 16 + i
            p[a * 32 + 16 + i] = a * 32 + i
    return p


def _perm_cols():
    p64 = _perm64()
    cols = []
    for (base, nh) in ((O_AQ, 8), (O_AK, 8), (O_CQ, 8), (O_CK, 2)):
        for hh in range(nh):
            cols.extend((base + hh * 64 + p64).tolist())
    return np.asarray(cols, np.int64)


def prep_shared(inp):
    f = np.float32
    sh = {}
    sh["w_ada"] = np.ascontiguousarray(inp["w_ada"], dtype=f)
    sh["b_adaT"] = np.ascontiguousarray(inp["b_ada"].reshape(L_ALL, 72, 128).transpose(2, 0, 1), dtype=f)
    sh["normgT"] = np.ascontiguousarray(inp["norm_g"].reshape(L_ALL, 3, 8, 128).transpose(3, 0, 1, 2), dtype=f)
    sh["fgT"] = np.ascontiguousarray(inp["final_g"].reshape(8, 128).T, dtype=f)
    for k in ("w_ffn1_in", "w_ffn1_out", "w_ffn2_in", "w_ffn2_out", "w_mix_in", "w_br_a", "w_br_b", "w_br_c", "w_mix_out"):
        sh[k] = np.ascontiguousarray(inp[k], dtype=f)
    p64 = _perm64()
    perm128 = np.concatenate([p64, 64 + p64])
    permT = np.zeros((128, 128), f)
    permT[perm128, np.arange(128)] = 1.0
    sh["permT"] = permT
    C, S = _rope_tables_np()
    sh["ropeC"] = C
    sh["ropeS"] = S
    sh["dlam"] = np.ascontiguousarray(np.broadcast_to(inp["diff_lambda"][None], (128, L_ALL, 4, 64)), dtype=f)
    sh["sublnT"] = np.ascontiguousarray(inp["diff_subln"].T, dtype=f)
    sh["gnT"] = np.ascontiguousarray(inp["gla_norm"].T, dtype=f)
    gw2 = np.zeros((L_ALL, 64, 512), f)
    gw2[:, 0:16, 0:256] = inp["gla_gate_w"][:, 0]
    gw2[:, 16:32, 256:512] = inp["gla_gate_w"][:, 1]
    gw2[:, 32, 0:256] = inp["gla_gate_b"][:, 0]
    gw2[:, 32, 256:512] = inp["gla_gate_b"][:, 1]
    sh["gw2"] = gw2
    sh["sinkB"] = np.ascontiguousarray(np.broadcast_to(inp["swa_sink"][None], (128, L_ALL, 8)), dtype=f)
    s = np.arange(128)[:, None]
    t = np.arange(128)[None, :]
    tri = np.stack([(s <= t), (s >= t), (s > t), (s < t)], axis=1).astype(f)
    sh["tri"] = np.ascontiguousarray(tri)
    return sh


def prep_core(inp, b):
    f = np.float32
    m = {}
    m["xc"] = np.ascontiguousarray(np.concatenate([inp["x"][b].T, inp["ctx"][b].T], axis=1), dtype=f)
    sc = np.stack([inp["c"][b].reshape(8, 128).T, inp["c_ctx"].reshape(8, 128).T], axis=-1)
    m["scT"] = np.ascontiguousarray(sc, dtype=f)
    return m


_NC_CACHE = {}


def kernel(**inputs):
    inp = {k: np.asarray(v) for k, v in inputs.items()}
    if "full" not in _NC_CACHE:
        _NC_CACHE["full"] = build()
    nc = _NC_CACHE["full"]
    sh = prep_shared(inp)
    in_maps = []
    for b in range(8):
        m = dict(sh)
        m.update(prep_core(inp, b))
        in_maps.append(m)
    res = run_bass_kernel_spmd(nc, in_maps, core_ids=list(range(8)))
    out = np.stack([np.ascontiguousarray(res.results[b]["out"].T) for b in range(8)], axis=0)
    return out.astype(np.float32)
```

```python
import math
from contextlib import ExitStack
import numpy as np
import concourse.bass as bass
import concourse.mybir as mybir
from concourse.bass_utils import run_bass_kernel_spmd

F32 = mybir.dt.float32
BF16 = mybir.dt.bfloat16
ALU = mybir.AluOpType
AF = mybir.ActivationFunctionType

D = 1024
SEQ = 4096
CTX = 256
NT = SEQ + CTX
L_ALL = 4
FFN = 2816
INC = 6944
EPS = 1e-6
O_AQ, O_AK, O_AV = 0, 512, 1024
O_BQ, O_BK, O_BV, O_LR, O_R = 1536, 1792, 2048, 2560, 2592
O_CQ, O_CK, O_CV, O_G = 3104, 3616, 3744, 3872
P_AQ, P_AK, P_CQ, P_CK = 0, 512, 1024, 1536
NPERM = 1664


class Buf:
    __slots__ = ("name", "w", "r")

    def __init__(self, name=""):
        self.name = name
        self.w = {}
        self.r = {}


class Eng:
    def __init__(self, fw, name, handle):
        self.name = name
        self.h = handle
        self.sem = fw.new_sem("e_" + name)
        self.cnt = 0
        self.known = {}

    def wait_tok(self, sem, val):
        k = id(sem)
        if self.known.get(k, 0) >= val:
            return
        self.h.wait_ge(sem, val)
        self.known[k] = val


class T:
    __slots__ = ("t", "b")

    def __init__(self, t, b):
        self.t = t
        self.b = b

    def __getitem__(self, idx):
        return self.t[idx]


class Ring:
    def __init__(self, items):
        self.items = items
        self.i = 0

    def next(self):
        t = self.items[self.i]
        self.i = (self.i + 1) % len(self.items)
        return t


class FW:
    def __init__(self, nc, n_dma_sems=16):
        self.nc = nc
        self.stacks = [ExitStack()]
        self.pe = Eng(self, "pe", nc.tensor)
        self.dve = Eng(self, "dve", nc.vector)
        self.act = Eng(self, "act", nc.scalar)
        self.pool = Eng(self, "pool", nc.gpsimd)
        self.sp = Eng(self, "sp", nc.sync)
        self.engs = [self.pe, self.dve, self.act, self.pool, self.sp]
        self.dsems = {}
        for e in (self.sp, self.pool, self.act):
            self.dsems[e.name] = [[self.new_sem("d_%s%d" % (e.name, i)), 0] for i in range(n_dma_sems)]
        self.dnext = {e: 0 for e in self.dsems}
        self.bar_sem = self.new_sem("bar")
        self.bar_cnt = 0
        self.n_inst = 0
        self.uid = 0

    def new_sem(self, name):
        return self.stacks[0].enter_context(self.nc.semaphore(name))

    def sbuf(self, name, shape, dtype):
        self.uid += 1
        t = self.stacks[-1].enter_context(self.nc.sbuf_tensor("%s_%d" % (name, self.uid), list(shape), dtype))
        return T(t, Buf(name))

    def psum(self, name, shape, dtype=F32):
        t = self.stacks[-1].enter_context(self.nc.psum_tensor(name, list(shape), dtype))
        return T(t, Buf(name))

    def ring(self, name, shape, dtype, n):
        return Ring([self.sbuf("%s%d" % (name, i), shape, dtype) for i in range(n)])

    def _deps(self, eng, reads, writes, self_sync):
        for b in reads:
            for (sem, val) in b.w.values():
                if sem is eng.sem and not self_sync:
                    continue
                eng.wait_tok(sem, val)
        for b in writes:
            for (sem, val) in b.w.values():
                if sem is eng.sem and not self_sync:
                    continue
                eng.wait_tok(sem, val)
            for (sem, val) in b.r.values():
                if sem is eng.sem and not self_sync:
                    continue
                eng.wait_tok(sem, val)

    def op(self, eng, fn, reads, writes, self_sync=True):
        self._deps(eng, reads, writes, self_sync)
        ins = fn()
        self.n_inst += 1
        eng.cnt += 1
        ins.then_inc(eng.sem, 1)
        tok = (eng.sem, eng.cnt)
        k = id(eng.sem)
        for b in reads:
            b.r[k] = tok
        for b in writes:
            b.w[k] = tok
            b.r = {}
        return ins

    def dma(self, eng, out_ap, in_ap, reads, writes):
        pool = self.dsems[eng.name]
        i = self.dnext[eng.name]
        self.dnext[eng.name] = (i + 1) % len(pool)
        slot = pool[i]
        sem = slot[0]
        if slot[1] > 0:
            eng.wait_tok(sem, slot[1])
        self._deps(eng, reads, writes, True)
        ins = eng.h.dma_start(out=out_ap, in_=in_ap)
        self.n_inst += 1
        slot[1] += 16
        ins.then_inc(sem, 16)
        tok = (sem, slot[1])
        k = id(sem)
        for b in reads:
            b.r[k] = tok
        for b in writes:
            b.w[k] = tok
            b.r = {}
        return ins

    def barrier(self):
        sp = self.sp
        for e in self.engs:
            if e is not sp and e.cnt > 0:
                sp.wait_tok(e.sem, e.cnt)
        for pool in self.dsems.values():
            for (sem, val) in pool:
                if val > 0:
                    sp.wait_tok(sem, val)
        self.bar_cnt += 1
        sp.h.sem_inc(self.bar_sem, 1)
        for e in self.engs:
            if e is not sp:
                e.wait_tok(self.bar_sem, self.bar_cnt)
        for e in self.engs:
            for e2 in self.engs:
                e.known[id(e2.sem)] = e2.cnt
            for pool in self.dsems.values():
                for (sem, val) in pool:
                    e.known[id(sem)] = val

    def phase(self):
        fw = self

        class _P:
            def __enter__(s):
                fw.stacks.append(ExitStack())

            def __exit__(s, *a):
                if a[0] is None:
                    fw.barrier()
                st = fw.stacks.pop()
                st.close()
                return False
        return _P()

    def close(self):
        self.stacks[0].close()


class DT:
    def __init__(self, nc, name, rows, cols, dtype, kind="Internal", rb=128, cb=128):
        self.ap = nc.dram_tensor(name, [rows, cols], dtype, kind=kind).ap()
        self.rb, self.cb = rb, cb
        self.bufs = {}
        self.name = name

    def B(self, r0, r1, c0, c1):
        out = []
        for i in range(r0 // self.rb, (r1 - 1) // self.rb + 1):
            for j in range(c0 // self.cb, (c1 - 1) // self.cb + 1):
                b = self.bufs.get((i, j))
                if b is None:
                    b = self.bufs[(i, j)] = Buf(self.name)
                out.append(b)
        return out


STS = [(0, 1024), (1024, 1024), (2048, 1024), (3072, 1024), (4096, 256)]


def lam_init_of(l):
    return 0.8 - 0.6 * math.exp(-0.3 * l)


B2CUT = [99]


def build(n_layers=L_ALL, stop_after=None, dump=None, mixers="ABC"):
    nc = bass.Bass("TRN2", target_bir_lowering=False)
    fw = FW(nc)
    V, A, G, PE, SP = fw.dve, fw.act, fw.pool, fw.pe, fw.sp
    nv, na, ng, nt = nc.vector, nc.scalar, nc.gpsimd, nc.tensor

    def din(name, shape, dtype=F32):
        return nc.dram_tensor(name, list(shape), dtype, kind="ExternalInput").ap()

    xc_in = din("xc", [D, NT])
    scT_in = din("scT", [128, 8, 2])
    w_ada = din("w_ada", [L_ALL, D, 9 * D])
    b_adaT = din("b_adaT", [128, L_ALL, 72])
    normgT = din("normgT", [128, L_ALL, 3, 8])
    fgT = din("fgT", [128, 8])
    w_ffn_in = [din("w_ffn1_in", [L_ALL, D, 2 * FFN]), din("w_ffn2_in", [L_ALL, D, 2 * FFN])]
    w_ffn_out = [din("w_ffn1_out", [L_ALL, FFN, D]), din("w_ffn2_out", [L_ALL, FFN, D])]
    w_mix_in = din("w_mix_in", [L_ALL, D, INC])
    w_perm = din("w_perm", [L_ALL, D, NPERM])
    w_br = [din("w_br_a", [L_ALL, 512, D]), din("w_br_b", [L_ALL, 512, D]), din("w_br_c", [L_ALL, 512, D])]
    w_mix_out = din("w_mix_out", [L_ALL, D, D])
    ropeC = din("ropeC", [128, NT])
    ropeS = din("ropeS", [128, NT])
    dlam = din("dlam", [128, L_ALL, 4, 64])
    sublnT = din("sublnT", [128, L_ALL])
    gnT = din("gnT", [128, L_ALL])
    gw2_in = din("gw2", [L_ALL, 64, 512])
    sinkB = din("sinkB", [128, L_ALL, 8])
    tri_in = din("tri", [128, 4, 128])
    out_dt = DT(nc, "out", D, SEQ, F32, kind="ExternalOutput")

    xs = DT(nc, "xs", D, NT, F32)
    h2s = DT(nc, "h2s", D, NT, BF16)
    Aq = DT(nc, "Aq", 512, NT, BF16); Ak = DT(nc, "Ak", 512, NT, BF16); Av = DT(nc, "Av", NT, 512, BF16)
    Bq = DT(nc, "Bq", 256, NT, BF16); Bk = DT(nc, "Bk", 256, NT, BF16); BkT = DT(nc, "BkT", NT, 256, BF16)
    Bv = DT(nc, "Bv", NT, 512, BF16); Blr = DT(nc, "Blr", 32, NT, BF16, rb=32); Br = DT(nc, "Br", 512, NT, BF16)
    Cq = DT(nc, "Cq", 512, NT, BF16); Ck = DT(nc, "Ck", 128, NT, BF16); Cv = DT(nc, "Cv", NT, 128, BF16)
    Oa = DT(nc, "Oa", 512, NT, BF16); Ob = DT(nc, "Ob", 512, NT, BF16); Oc = DT(nc, "Oc", 512, NT, BF16, rb=64)

    def wscr(name, shape):
        return (nc.dram_tensor(name, list(shape), BF16, kind="Internal").ap(), Buf(name))
    WB = []
    for l in range(n_layers):
        WB.append(dict(
            f_in=[wscr("wb_f1i_%d" % l, [D, 2 * FFN]), wscr("wb_f2i_%d" % l, [D, 2 * FFN])],
            f_out=[wscr("wb_f1o_%d" % l, [FFN, D]), wscr("wb_f2o_%d" % l, [FFN, D])],
            mix=wscr("wb_mix_%d" % l, [D, INC]), perm=wscr("wb_perm_%d" % l, [D, NPERM]),
            br=[wscr("wb_br%d_%d" % (i, l), [512, D]) for i in range(3)],
            out=wscr("wb_out_%d" % l, [D, D])))

    def convert_layer(l, part, after=()):
        w = WB[l]
        if part == 0:
            fw.dma(G, w["f_in"][0][0], w_ffn_in[0][l], list(after), [w["f_in"][0][1]])
            fw.dma(G, w["f_out"][0][0], w_ffn_out[0][l], list(after), [w["f_out"][0][1]])
            return
        fw.dma(G, w["mix"][0], w_mix_in[l], list(after), [w["mix"][1]])
        fw.dma(G, w["perm"][0], w_perm[l], list(after), [w["perm"][1]])
        for i in range(3):
            fw.dma(G, w["br"][i][0], w_br[i][l], list(after), [w["br"][i][1]])
        fw.dma(G, w["out"][0], w_mix_out[l], list(after), [w["out"][1]])
        fw.dma(G, w["f_in"][1][0], w_ffn_in[1][l], list(after), [w["f_in"][1][1]])
        fw.dma(G, w["f_out"][1][0], w_ffn_out[1][l], list(after), [w["f_out"][1][1]])

    ones_bf = fw.sbuf("ones_bf", [128, 128], BF16)
    ones_f = fw.sbuf("ones_f", [128, 2], F32)
    tri_f = fw.sbuf("tri_f", [128, 4, 128], F32)
    tri_b = fw.sbuf("tri_b", [128, 4, 128], BF16)
    modT = fw.sbuf("modT", [128, L_ALL, 9, 8, 2], F32)
    gsT = fw.sbuf("gsT", [128, L_ALL, 3, 8, 2], F32)
    cfT = fw.sbuf("cfT", [128, L_ALL, 3, 8, 2], F32)
    ngT = fw.sbuf("ngT", [128, L_ALL, 3, 8], F32)
    fg = fw.sbuf("fg", [128, 8], F32)
    neglam = fw.sbuf("neglam", [128, L_ALL], F32)
    subl = fw.sbuf("subl", [128, L_ALL], F32)
    gn = fw.sbuf("gn", [128, L_ALL], F32)
    esink = fw.sbuf("esink", [128, L_ALL, 8], F32)
    PSB = [fw.psum("ps%d" % i, [128, 512], F32) for i in range(8)]
    PS = Ring(PSB)

    fw.op(V, lambda: nv.memset(ones_bf[:], 1.0), [], [ones_bf.b])
    fw.op(V, lambda: nv.memset(ones_f[:], 1.0), [], [ones_f.b])
    ones_ff = fw.sbuf("ones_ff", [128, 128], F32)
    fw.op(V, lambda: nv.memset(ones_ff[:], 1.0), [], [ones_ff.b])
    eps_t = fw.sbuf("eps_t", [128, 2], F32)
    fw.op(V, lambda: nv.memset(eps_t[:], EPS), [], [eps_t.b])
    fw.dma(SP, tri_f[:], tri_in, [], [tri_f.b])
    fw.dma(G, tri_b[:], tri_in, [], [tri_b.b])
    fw.dma(SP, ngT[:], normgT, [], [ngT.b])
    fw.dma(SP, fg[:], fgT, [], [fg.b])
    fw.dma(SP, subl[:], sublnT, [], [subl.b])
    fw.dma(SP, gn[:], gnT, [], [gn.b])
    xin_b = Buf("xin")
    for (t0, n) in STS:
        fw.dma(SP, xs.ap[:, t0:t0 + n], xc_in[:, t0:t0 + n], [], xs.B(0, D, t0, t0 + n))
    convert_layer(0, 0)
    convert_layer(0, 1)

    with fw.phase():
        sc = fw.sbuf("sc", [128, 8, 2], F32)
        sg = fw.sbuf("sg", [128, 8, 2], F32)
        badd = fw.sbuf("badd", [128, L_ALL, 72], F32)
        fw.dma(SP, sc[:], scT_in, [], [sc.b])
        fw.dma(SP, badd[:], b_adaT, [], [badd.b])
        fw.op(A, lambda: na.activation(out=sg[:], in_=sc[:], func=AF.Sigmoid), [sc.b], [sg.b])
        fw.op(V, lambda: nv.tensor_tensor(out=sc[:], in0=sc[:], in1=sg[:], op=ALU.mult), [sc.b, sg.b], [sc.b])
        wa_r = fw.ring("wa", [128, 4608], F32, 3)
        for l in range(n_layers):
            for half in range(2):
                acc = PS.next()
                for kc in range(8):
                    wt = wa_r.next()
                    fw.dma(SP, wt[:], w_ada[l, kc * 128:(kc + 1) * 128, half * 4608:(half + 1) * 4608], [], [wt.b])
                    for j in range(36):
                        fw.op(PE, lambda: nt.matmul(acc[:, 2 * j:2 * j + 2], lhsT=wt[:, j * 128:(j + 1) * 128],
                                                    rhs=sc[:, kc, :], start=(kc == 0 and j == 0), stop=(kc == 7),
                                                    skip_group_check=True),
                              [wt.b, sc.b], [acc.b], self_sync=False)
                mv = modT[:, l].rearrange("p i c w -> p (i c) w")[:, half * 36:(half + 1) * 36, :]
                fw.op(V, lambda: nv.tensor_tensor(
                    out=mv, in0=acc[:, 0:72].rearrange("p (j w) -> p j w", w=2),
                    in1=badd[:, l, half * 36:(half + 1) * 36].unsqueeze(2).to_broadcast([128, 36, 2]), op=ALU.add),
                    [acc.b, badd.b], [modT.b])
            for i in range(3):
                fw.op(V, lambda: nv.scalar_tensor_tensor(
                    out=gsT[:, l, i], in0=modT[:, l, 3 * i + 1], scalar=1.0,
                    in1=ngT[:, l, i].unsqueeze(2).to_broadcast([128, 8, 2]), op0=ALU.add, op1=ALU.mult),
                    [modT.b, ngT.b], [gsT.b])
                fw.op(V, lambda: nv.tensor_scalar(out=cfT[:, l, i], in0=modT[:, l, 3 * i + 2],
                                                  scalar1=(1.0 if i == 1 else 0.5), scalar2=None, op0=ALU.mult),
                      [modT.b], [cfT.b])
        dl = fw.sbuf("dl", [128, L_ALL, 4, 64], F32)
        pr = fw.sbuf("pr", [128, L_ALL, 2, 64], F32)
        sm = fw.sbuf("sm", [128, L_ALL, 2], F32)
        fw.dma(SP, dl[:], dlam, [], [dl.b])
        for l in range(n_layers):
            for j in range(2):
                fw.op(V, lambda: nv.tensor_tensor(out=pr[:, l, j], in0=dl[:, l, 2 * j], in1=dl[:, l, 2 * j + 1], op=ALU.mult),
                      [dl.b], [pr.b])
                fw.op(V, lambda: nv.reduce_sum(out=sm[:, l, j:j + 1], in_=pr[:, l, j], axis=mybir.AxisListType.X),
                      [pr.b], [sm.b])
        fw.op(A, lambda: na.activation(out=sm[:], in_=sm[:], func=AF.Exp), [sm.b], [sm.b])
        for l in range(n_layers):
            fw.op(V, lambda: nv.tensor_tensor(out=neglam[:, l:l + 1], in0=sm[:, l, 1:2], in1=sm[:, l, 0:1], op=ALU.subtract),
                  [sm.b], [neglam.b])
            fw.op(V, lambda: nv.tensor_scalar(out=neglam[:, l:l + 1], in0=neglam[:, l:l + 1], scalar1=-lam_init_of(l),
                                              scalar2=None, op0=ALU.add), [neglam.b], [neglam.b])
            fw.op(V, lambda: nv.tensor_scalar(out=subl[:, l:l + 1], in0=subl[:, l:l + 1], scalar1=1.0 - lam_init_of(l),
                                              scalar2=None, op0=ALU.mult), [subl.b], [subl.b])
        sk = fw.sbuf("sk", [128, L_ALL, 8], F32)
        fw.dma(SP, sk[:], sinkB, [], [sk.b])
        fw.op(A, lambda: na.activation(out=esink[:], in_=sk[:], func=AF.Exp), [sk.b], [esink.b])

    def mcol(t, l, i, kc, isctx):
        w = 1 if isctx else 0
        return t[:, l, i, kc, w:w + 1]

    def rstd_from(rs, acc, n, inv_cnt, np_=128):
        fw.op(V, lambda: nv.tensor_scalar(out=rs[0:np_, 0:n], in0=acc[0:np_, 0:n], scalar1=inv_cnt, scalar2=EPS,
                                          op0=ALU.mult, op1=ALU.add), [acc.b], [rs.b])
        fw.op(A, lambda: na.activation(out=rs[0:np_, 0:n], in_=rs[0:np_, 0:n], func=AF.Sqrt), [rs.b], [rs.b])
        fw.op(V, lambda: nv.reciprocal(out=rs[0:np_, 0:n], in_=rs[0:np_, 0:n]), [rs.b], [rs.b])

    def norm_mod(l, ni, src, t0, n, isctx, h, hoff, rings, plain_scale=None):
        xt_r, sq_r, rs_r, tm_r = rings
        xt = xt_r.next()
        fw.dma(SP, xt[:, :, 0:n], src.ap[:, t0:t0 + n].rearrange("(k p) n -> p k n", p=128), src.B(0, D, t0, t0 + n), [xt.b])
        acc = PS.next()
        for kc in range(8):
            sq = sq_r.next()
            fw.op(A, lambda: na.activation(out=sq[:, 0:n], in_=xt[:, kc, 0:n], func=AF.Square), [xt.b], [sq.b])
            fw.op(PE, lambda: nt.matmul(acc[:, 0:n], lhsT=ones_bf[:], rhs=sq[:, 0:n], start=(kc == 0), stop=(kc == 7)),
                  [sq.b, ones_bf.b], [acc.b], self_sync=False)
        rs = rs_r.next()
        rstd_from(rs, acc, n, 1.0 / D)
        return xt, rs

    def apply_mod(l, ni, xt, rs, n, isctx, h, hoff, tm_r):
        for kc in range(8):
            tm = tm_r.next()
            fw.op(V, lambda: nv.scalar_tensor_tensor(out=tm[:, 0:n], in0=xt[:, kc, 0:n], scalar=mcol(gsT, l, ni, kc, isctx),
                                                     in1=rs[:, 0:n], op0=ALU.mult, op1=ALU.mult),
                  [xt.b, rs.b, gsT.b], [tm.b])
            fw.op(A, lambda: na.activation(out=h[:, kc, hoff:hoff + n], in_=tm[:, 0:n], func=AF.Identity,
                                           bias=modT[:, l, 3 * ni, kc, (1 if isctx else 0):(2 if isctx else 1)]),
                  [tm.b, modT.b], [h.b])

    def ld_w(eng, wt_ap, wsrc, rows0, nk, c0, ncols, wt_T):
        ap, b = wsrc
        fw.dma(eng, wt_ap, ap[rows0:rows0 + nk * 128, c0:c0 + ncols].rearrange("(k p) c -> p k c", p=128), [b], [wt_T.b])

    def residual_out(l, ci, pt, oc, t0, n, isctx, xr_r, xo_r, dst=xs):
        xr = xr_r.next()
        fw.dma(SP, xr[:, 0:n], xs.ap[oc * 128:(oc + 1) * 128, t0:t0 + n], xs.B(oc * 128, (oc + 1) * 128, t0, t0 + n), [xr.b])
        xo = xo_r.next()
        fw.op(V, lambda: nv.scalar_tensor_tensor(out=xo[:, 0:n], in0=pt[:, 0:n], scalar=mcol(cfT, l, ci, oc, isctx),
                                                 in1=xr[:, 0:n], op0=ALU.mult, op1=ALU.add),
              [pt.b, xr.b, cfT.b], [xo.b])
        fw.dma(SP, xs.ap[oc * 128:(oc + 1) * 128, t0:t0 + n], xo[:, 0:n], [xo.b], xs.B(oc * 128, (oc + 1) * 128, t0, t0 + n))

    def tiles_of(t0, n):
        return [(t0 + i, min(512, n - i)) for i in range(0, n, 512)]

    def ffn_phase(l, which, skip_ctx=False):
        ni = 0 if which == 0 else 2
        W1 = WB[l]["f_in"][which]
        W2 = WB[l]["f_out"][which]
        with fw.phase():
            h = fw.sbuf("h", [128, 8, 1024], BF16)
            g = fw.sbuf("g", [128, 22, 1024], BF16)
            rings = (fw.ring("xt", [128, 8, 256], F32, 2), fw.ring("sq", [128, 256], BF16, 3),
                     fw.ring("rs", [128, 256], F32, 2), fw.ring("tm", [128, 256], F32, 3))
            wu_r = fw.ring("wu", [128, 8, 2, 512], BF16, 2)
            w2_r = fw.ring("w2", [128, 22, 256], BF16, 2)
            s_r = fw.ring("s", [128, 512], F32, 3)
            xr_r = fw.ring("xr", [128, 512], F32, 3)
            xo_r = fw.ring("xo", [128, 512], F32, 3)
            sts = [(t0, n) for (t0, n) in STS if not (t0 >= SEQ and skip_ctx)]

            def do_norm(t0, n):
                for off in range(0, n, 256):
                    xt, rs = norm_mod(l, ni, xs, t0 + off, 256, t0 >= SEQ, h, off, rings)
                    apply_mod(l, ni, xt, rs, 256, t0 >= SEQ, h, off, rings[3])

            def ld_w1(j0):
                nj = min(4, 22 - j0)
                wt = wu_r.next()
                ld_w(SP, wt[:, :, 0, 0:nj * 128], W1, 0, 8, j0 * 128, nj * 128, wt)
                ld_w(SP, wt[:, :, 1, 0:nj * 128], W1, 0, 8, FFN + j0 * 128, nj * 128, wt)
                return wt

            def ld_w2(op2):
                w2 = w2_r.next()
                ld_w(SP, w2[:], W2, 0, 22, op2 * 256, 256, w2)
                return w2

            do_norm(*sts[0])
            for si, (t0, n) in enumerate(sts):
                isctx = t0 >= SEQ
                tls = tiles_of(0, n)
                groups = list(range(0, 22, 4))
                if si == 0:
                    nxt_w = ld_w1(groups[0])
                for gi_, j0 in enumerate(groups):
                    nj = min(4, 22 - j0)
                    wt = nxt_w
                    if gi_ + 1 < len(groups):
                        nxt_w = ld_w1(groups[gi_ + 1])
                    else:
                        nxt_w2 = ld_w2(0)
                    for jj in range(nj):
                        for (o, m) in tls:
                            pu = PS.next()
                            for kc in range(8):
                                fw.op(PE, lambda: nt.matmul(pu[:, 0:m], lhsT=wt[:, kc, 0, jj * 128:(jj + 1) * 128], rhs=h[:, kc, o:o + m],
                                                            start=(kc == 0), stop=(kc == 7)), [wt.b, h.b], [pu.b], self_sync=False)
                            pv = PS.next()
                            for kc in range(8):
                                fw.op(PE, lambda: nt.matmul(pv[:, 0:m], lhsT=wt[:, kc, 1, jj * 128:(jj + 1) * 128], rhs=h[:, kc, o:o + m],
                                                            start=(kc == 0), stop=(kc == 7)), [wt.b, h.b], [pv.b], self_sync=False)
                            s = s_r.next()
                            fw.op(A, lambda: na.activation(out=s[:, 0:m], in_=pu[:, 0:m], func=AF.Silu), [pu.b], [s.b])
                            fw.op(V, lambda: nv.tensor_tensor(out=g[:, j0 + jj, o:o + m], in0=s[:, 0:m], in1=pv[:, 0:m], op=ALU.mult),
                                  [s.b, pv.b], [g.b])
                for op2 in range(4):
                    w2 = nxt_w2
                    if op2 + 1 < 4:
                        nxt_w2 = ld_w2(op2 + 1)
                    elif si + 1 < len(sts):
                        nxt_w = ld_w1(groups[0])
                    if op2 == 1 and si + 1 < len(sts):
                        do_norm(*sts[si + 1])
                    for oo in range(2):
                        oc = op2 * 2 + oo
                        for (o, m) in tls:
                            pt = PS.next()
                            for j in range(22):
                                fw.op(PE, lambda: nt.matmul(pt[:, 0:m], lhsT=w2[:, j, oo * 128:(oo + 1) * 128], rhs=g[:, j, o:o + m],
                                                            start=(j == 0), stop=(j == 21)), [w2.b, g.b], [pt.b], self_sync=False)
                            residual_out(l, 0 if which == 0 else 2, pt, oc, t0 + o, m, isctx, xr_r, xo_r)

    def mixin_phase(l):
        Wm = WB[l]["mix"]
        Wp = WB[l]["perm"]
        with fw.phase():
            h = fw.sbuf("h2", [128, 8, 1024], BF16)
            rings = (fw.ring("xt", [128, 8, 256], F32, 2), fw.ring("sq", [128, 256], BF16, 3),
                     fw.ring("rs", [128, 256], F32, 2), fw.ring("tm", [128, 256], F32, 3))
            w_r = fw.ring("wm", [128, 8, 512], BF16, 3)
            wp_r = fw.ring("wp", [128, 8, 512], BF16, 3)
            rc = fw.sbuf("rc", [128, 1024], F32)
            rsn = fw.sbuf("rsn", [128, 1024], F32)
            t1_r = fw.ring("t1", [128, 512], F32, 2)
            t2_r = fw.ring("t2", [128, 512], F32, 2)
            ob_r = fw.ring("ob", [128, 512], BF16, 4)
            FMS = [("rope", O_AQ, P_AQ, 4, Aq), ("rope", O_AK, P_AK, 4, Ak), ("plain", O_BQ, None, 2, Bq),
                   ("plain", O_BK, None, 2, Bk), ("silu", O_R, None, 4, Br), ("rope", O_CQ, P_CQ, 4, Cq),
                   ("rope", O_CK, P_CK, 1, Ck)]
            TMS = [(O_AV, 512, Av), (O_BK, 256, BkT), (O_BV, 512, Bv), (O_CV, 128, Cv)]
            flip = [0]
            h_r = Ring([h, fw.sbuf("h2b", [128, 8, 1024], BF16)])

            def do_norm(t0, n, hh_):
                for off in range(0, n, 256):
                    xt, rs = norm_mod(l, 1, xs, t0 + off, 256, t0 >= SEQ, hh_, off, rings)
                    apply_mod(l, 1, xt, rs, 256, t0 >= SEQ, hh_, off, rings[3])

            def ld_seg(seg):
                if seg[0] == "fm":
                    (kind, c0, pc0, nch, dst) = seg[1]
                    wt = w_r.next()
                    ld_w(SP, wt[:, :, 0:nch * 128], Wm, 0, 8, c0, nch * 128, wt)
                    wp = None
                    if kind == "rope":
                        wp = wp_r.next()
                        ld_w(SP, wp[:, :, 0:nch * 128], Wp, 0, 8, pc0, nch * 128, wp)
                    return (wt, wp)
                if seg[0] == "lr":
                    wt = w_r.next()
                    ld_w(SP, wt[:, :, 0:32], Wm, 0, 8, O_LR, 32, wt)
                    return (wt, None)
                (c0, ncols, dst) = seg[1]
                wt = w_r.next()
                ld_w(SP, wt[:, :, 0:ncols], Wm, 0, 8, c0, ncols, wt)
                return (wt, None)

            segs = [("fm", f_) for f_ in FMS] + [("lr", None)] + [("tm", t_) for t_ in TMS]
            h = h_r.next()
            do_norm(STS[0][0], STS[0][1], h)
            for si, (t0, n) in enumerate(STS):
                isctx = t0 >= SEQ
                h_next = None
                fw.dma(SP, h2s.ap[:, t0:t0 + n].rearrange("(k p) n -> p k n", p=128), h[:, :, 0:n], [h.b], h2s.B(0, D, t0, t0 + n))
                fw.dma(SP, rc[:, 0:n], ropeC[:, t0:t0 + n], [], [rc.b])
                fw.dma(SP, rsn[:, 0:n], ropeS[:, t0:t0 + n], [], [rsn.b])
                tls = tiles_of(0, n)
                if si == 0:
                    nxt = ld_seg(segs[0])
                for k, seg in enumerate(segs):
                    (wt, wp) = nxt
                    if k + 1 < len(segs):
                        nxt = ld_seg(segs[k + 1])
                    elif si + 1 < len(STS):
                        nxt = ld_seg(segs[0])
                    if k == 5 and si + 1 < len(STS):
                        h_next = h_r.next()
                        do_norm(STS[si + 1][0], STS[si + 1][1], h_next)
                    if seg[0] == "fm":
                        (kind, c0, pc0, nch, dst) = seg[1]
                        for j in range(nch):
                            for (o, m) in tls:
                                p1 = PS.next()
                                for kc in range(8):
                                    fw.op(PE, lambda: nt.matmul(p1[:, 0:m], lhsT=wt[:, kc, j * 128:(j + 1) * 128], rhs=h[:, kc, o:o + m],
                                                                start=(kc == 0), stop=(kc == 7)), [wt.b, h.b], [p1.b], self_sync=False)
                                ob = ob_r.next()
                                if kind == "rope":
                                    p2 = PS.next()
                                    for kc in range(8):
                                        fw.op(PE, lambda: nt.matmul(p2[:, 0:m], lhsT=wp[:, kc, j * 128:(j + 1) * 128], rhs=h[:, kc, o:o + m],
                                                                    start=(kc == 0), stop=(kc == 7)), [wp.b, h.b], [p2.b], self_sync=False)
                                    t1 = t1_r.next()
                                    t2 = t2_r.next()
                                    fw.op(V, lambda: nv.tensor_tensor(out=t1[:, 0:m], in0=p1[:, 0:m], in1=rc[:, o:o + m], op=ALU.mult),
                                          [p1.b, rc.b], [t1.b])
                                    fw.op(V, lambda: nv.tensor_tensor(out=t2[:, 0:m], in0=p2[:, 0:m], in1=rsn[:, o:o + m], op=ALU.mult),
                                          [p2.b, rsn.b], [t2.b])
                                    fw.op(G, lambda: ng.tensor_tensor(out=ob[:, 0:m], in0=t1[:, 0:m], in1=t2[:, 0:m], op=ALU.add),
                                          [t1.b, t2.b], [ob.b])
                                elif kind == "silu":
                                    fw.op(A, lambda: na.activation(out=ob[:, 0:m], in_=p1[:, 0:m], func=AF.Silu), [p1.b], [ob.b])
                                else:
                                    fw.op(A, lambda: na.activation(out=ob[:, 0:m], in_=p1[:, 0:m], func=AF.Copy), [p1.b], [ob.b])
                                fw.dma(SP, dst.ap[j * 128:(j + 1) * 128, t0 + o:t0 + o + m], ob[:, 0:m], [ob.b],
                                       dst.B(j * 128, (j + 1) * 128, t0 + o, t0 + o + m))
                    elif seg[0] == "lr":
                        for (o, m) in tls:
                            p1 = PS.next()
                            for kc in range(8):
                                fw.op(PE, lambda: nt.matmul(p1[0:32, 0:m], lhsT=wt[:, kc, 0:32], rhs=h[:, kc, o:o + m],
                                                            start=(kc == 0), stop=(kc == 7)), [wt.b, h.b], [p1.b], self_sync=False)
                            ob = ob_r.next()
                            fw.op(A, lambda: na.activation(out=ob[0:32, 0:m], in_=p1[0:32, 0:m], func=AF.Copy), [p1.b], [ob.b])
                            fw.dma(SP, Blr.ap[:, t0 + o:t0 + o + m], ob[0:32, 0:m], [ob.b], Blr.B(0, 32, t0 + o, t0 + o + m))
                    else:
                        (c0, ncols, dst) = seg[1]
                        for tb in range(n // 128):
                            p1 = PS.next()
                            for kc in range(8):
                                fw.op(PE, lambda: nt.matmul(p1[:, 0:ncols], lhsT=h[:, kc, tb * 128:(tb + 1) * 128], rhs=wt[:, kc, 0:ncols],
                                                            start=(kc == 0), stop=(kc == 7)), [wt.b, h.b], [p1.b], self_sync=False)
                            ob = ob_r.next()
                            flip[0] ^= 1
                            if flip[0]:
                                fw.op(A, lambda: na.activation(out=ob[:, 0:ncols], in_=p1[:, 0:ncols], func=AF.Copy), [p1.b], [ob.b])
                            else:
                                fw.op(V, lambda: nv.tensor_copy(out=ob[:, 0:ncols], in_=p1[:, 0:ncols]), [p1.b], [ob.b])
                            r0 = t0 + tb * 128
                            fw.dma(SP, dst.ap[r0:r0 + 128, 0:ncols], ob[:, 0:ncols], [ob.b], dst.B(r0, r0 + 128, 0, ncols))
                if h_next is not None:
                    h = h_next

    def mixA_phase(l, with_ctx, after_head1_loads=None):
        with fw.phase():
            K_r = fw.ring("Ka", [128, NT], BF16, 2)
            Q_r = fw.ring("Qa", [128, NT], BF16, 2)
            V_r = fw.ring("Va", [128, 34, 128], BF16, 2)
            p_r = fw.ring("Pa", [128, 512], BF16, 6)
            ev_r = fw.ring("eva", [128, 4, 512], F32, 2)
            f_r = fw.ring("fa", [128, 512], F32, 2)
            sq_r = fw.ring("sqa", [128, 512], BF16, 2)
            ob_r = fw.ring("oba", [128, 512], BF16, 2)
            Oacc = [PSB[0], PSB[2]]
            Dacc = [PSB[1], PSB[3]]
            SR = Ring(PSB[4:8])
            pending = []
            DEFER = 8
            dacc_r = fw.ring("dacc", [128, 512], F32, 2)
            dcur = [None]
            def ld_head(hd):
                Kt = K_r.next(); Qt = Q_r.next(); Vt = V_r.next()
                fw.dma(SP, Kt[:], Ak.ap[hd * 128:(hd + 1) * 128, :], Ak.B(hd * 128, (hd + 1) * 128, 0, NT), [Kt.b])
                fw.dma(SP, Qt[:], Aq.ap[hd * 128:(hd + 1) * 128, :], Aq.B(hd * 128, (hd + 1) * 128, 0, NT), [Qt.b])
                fw.dma(SP, Vt[:], Av.ap[:, hd * 128:(hd + 1) * 128].rearrange("(c p) v -> p c v", p=128),
                       Av.B(0, NT, hd * 128, (hd + 1) * 128), [Vt.b])
                return (Kt, Qt, Vt)

            nxt_head = ld_head(0)
            for hd in range(4):
                (Kt, Qt, Vt) = nxt_head
                if hd + 1 < 4:
                    nxt_head = ld_head(hd + 1)
                if hd == 0 and after_head1_loads is not None:
                    after_head1_loads([Kt.b, Qt.b, Vt.b, nxt_head[0].b, nxt_head[1].b, nxt_head[2].b])
                qtiles = [(q0, 512, list(range(34))) for q0 in range(0, SEQ, 512)]
                if with_ctx:
                    qtiles.append((SEQ, 256, [32, 33]))
                for (q0, m, kcs) in qtiles:
                    S_t = {}

                    def emit_qk(i):
                        kc = kcs[i]
                        for mm in range(2):
                            S = SR.next()
                            fw.op(PE, lambda: nt.matmul(S[:, 0:m], lhsT=Kt[mm * 64:(mm + 1) * 64, kc * 128:(kc + 1) * 128],
                                                        rhs=Qt[mm * 64:(mm + 1) * 64, q0:q0 + m], start=True, stop=True),
                                  [Kt.b, Qt.b], [S.b], self_sync=False)
                            S_t[(i, mm)] = S

                    def emit_pv(i):
                        kc = kcs[i]
                        first = (i == 0); lastk = (i == len(kcs) - 1)
                        for mm in range(2):
                            S = S_t.pop((i, mm))
                            P = p_r.next()
                            fw.op(A, lambda: na.activation(out=P[:, 0:m], in_=S[:, 0:m], func=AF.Exp, scale=0.125), [S.b], [P.b])
                            if mm == 0:
                                if first:
                                    dcur[0] = dacc_r.next()
                                    fw.op(V, lambda: nv.tensor_copy(out=dcur[0][:, 0:m], in_=P[:, 0:m]), [P.b], [dcur[0].b])
                                else:
                                    fw.op(V, lambda: nv.tensor_tensor(out=dcur[0][:, 0:m], in0=dcur[0][:, 0:m], in1=P[:, 0:m], op=ALU.add),
                                          [P.b, dcur[0].b], [dcur[0].b])
                            else:
                                fw.op(PE, lambda: nt.matmul(Dacc[mm][:, 0:m], lhsT=ones_bf[:], rhs=P[:, 0:m], start=first, stop=lastk),
                                      [P.b, ones_bf.b], [Dacc[mm].b], self_sync=False)
                            fw.op(PE, lambda: nt.matmul(Oacc[mm][:, 0:m], lhsT=Vt[:, kc, :], rhs=P[:, 0:m], start=first, stop=lastk),
                                  [P.b, Vt.b], [Oacc[mm].b], self_sync=False)
                    n_st = len(kcs)
                    pv_done = set()
                    for i in range(n_st + 1):
                        if i < n_st:
                            emit_qk(i)
                        if i >= 1 and (i - 1) not in pv_done:
                            emit_pv(i - 1)
                        if (i == DEFER or i == n_st) and pending:
                            if i < n_st:
                                emit_pv(i)
                                pv_done.add(i)
                            while pending:
                                pending.pop(0)()
                    fw.op(PE, lambda: nt.matmul(Dacc[0][:, 0:m], lhsT=ones_ff[:], rhs=dcur[0][:, 0:m], start=True, stop=True),
                          [dcur[0].b, ones_ff.b], [Dacc[0].b], self_sync=False)
                    ev = ev_r.next()
                    for mm_ in range(2):
                        fw.op(A, lambda: na.activation(out=ev[:, 1 + 2 * mm_, 0:m], in_=Dacc[mm_][:, 0:m], func=AF.Ln), [Dacc[mm_].b], [ev.b])
                        fw.op(A, lambda: na.activation(out=ev[:, 1 + 2 * mm_, 0:m], in_=ev[:, 1 + 2 * mm_, 0:m], func=AF.Exp, scale=-1.0), [ev.b], [ev.b])
                    fw.op(V, lambda: nv.tensor_tensor(out=ev[:, 0, 0:m], in0=Oacc[0][:, 0:m], in1=ev[:, 1, 0:m], op=ALU.mult), [Oacc[0].b, ev.b], [ev.b])
                    fw.op(V, lambda: nv.tensor_tensor(out=ev[:, 2, 0:m], in0=Oacc[1][:, 0:m], in1=ev[:, 3, 0:m], op=ALU.mult), [Oacc[1].b, ev.b], [ev.b])
                    fw.op(V, lambda: nv.scalar_tensor_tensor(out=ev[:, 0, 0:m], in0=ev[:, 2, 0:m], scalar=neglam[:, l:l + 1], in1=ev[:, 0, 0:m],
                                                             op0=ALU.mult, op1=ALU.add), [ev.b, neglam.b], [ev.b])
                    sq = sq_r.next()
                    fw.op(G, lambda: ng.tensor_tensor(out=sq[:, 0:m], in0=ev[:, 0, 0:m], in1=ev[:, 0, 0:m], op=ALU.mult), [ev.b], [sq.b])

                    def part2(ev=ev, sq=sq, m=m, q0=q0, hd=hd):
                        ss = SR.next()
                        fw.op(PE, lambda: nt.matmul(ss[:, 0:m], lhsT=ones_bf[:], rhs=sq[:, 0:m], start=True, stop=True),
                              [sq.b, ones_bf.b], [ss.b], self_sync=False)
                        rs = f_r.next()
                        fw.op(A, lambda: na.activation(out=rs[:, 0:m], in_=ss[:, 0:m], func=AF.Ln, scale=1.0 / 128, bias=eps_t[:, 0:1]), [ss.b, eps_t.b], [rs.b])
                        fw.op(A, lambda: na.activation(out=rs[:, 0:m], in_=rs[:, 0:m], func=AF.Exp, scale=-0.5), [rs.b], [rs.b])
                        ob = ob_r.next()
                        fw.op(V, lambda: nv.scalar_tensor_tensor(out=ob[:, 0:m], in0=ev[:, 0, 0:m], scalar=subl[:, l:l + 1], in1=rs[:, 0:m],
                                                                 op0=ALU.mult, op1=ALU.mult), [ev.b, rs.b, subl.b], [ob.b])
                        fw.dma(SP, Oa.ap[hd * 128:(hd + 1) * 128, q0:q0 + m], ob[:, 0:m], [ob.b], Oa.B(hd * 128, (hd + 1) * 128, q0, q0 + m))
                    pending.append(part2)
            while pending:
                pending.pop(0)()

    def mixC_phase(l, with_ctx):
        with fw.phase():
            Qg_r = fw.ring("Qg", [64, 4, NT], BF16, 2)
            Kc_r = fw.ring("Kc", [64, NT], BF16, 2)
            Vc_r = fw.ring("Vc", [128, 34, 64], BF16, 2)
            esr = fw.sbuf("esr", [64, 2, 512], F32)
            p_r = fw.ring("Pc", [128, 512], BF16, 6)
            f_r = fw.ring("fc", [64, 512], F32, 4)
            ob_r = fw.ring("obc", [64, 512], BF16, 3)
            Oacc = Ring([PSB[0], PSB[2]])
            Dacc = Ring([PSB[1], PSB[3]])
            SR = Ring(PSB[4:8])
            for hk in range(2):
                for hh in range(4):
                    fw.op(V, lambda: nv.tensor_copy(out=esr[:, hk, hh * 128:(hh + 1) * 128],
                                                    in_=esink[0:64, l, hk * 4 + hh:hk * 4 + hh + 1].to_broadcast([64, 128])),
                          [esink.b], [esr.b])
            def ld_group(hk):
                Qg = Qg_r.next(); Kc = Kc_r.next(); Vc = Vc_r.next()
                for hh in range(4):
                    r0 = (hk * 4 + hh) * 64
                    fw.dma(SP, Qg[:, hh, :], Cq.ap[r0:r0 + 64, :], Cq.B(r0, r0 + 64, 0, NT), [Qg.b])
                fw.dma(SP, Kc[:], Ck.ap[hk * 64:(hk + 1) * 64, :], Ck.B(hk * 64, (hk + 1) * 64, 0, NT), [Kc.b])
                fw.dma(SP, Vc[:], Cv.ap[:, hk * 64:(hk + 1) * 64].rearrange("(c p) v -> p c v", p=128),
                       Cv.B(0, NT, hk * 64, (hk + 1) * 64), [Vc.b])
                return (Qg, Kc, Vc)

            groups_c = [ld_group(0), ld_group(1)]
            for hk in range(2):
                (Qg, Kc, Vc) = groups_c[hk]
                blocks = list(range(32)) + ([32, 33] if with_ctx else [])
                flat = []
                for b in blocks:
                    if b < 32:
                        chunks = ([(b - 1, 1)] if b > 0 else []) + [(b, None)] + ([(b + 1, 0)] if b < 31 else []) + [(32, None), (33, None)]
                    else:
                        chunks = [(32, None), (33, None)]
                    for ci, (kc, mk) in enumerate(chunks):
                        flat.append((b, kc, mk, ci == 0, ci == len(chunks) - 1))
                S_t = {}
                acc = {}

                def emit_qk(i):
                    b, kc, mk, first, lastk = flat[i]
                    S = SR.next()
                    fw.op(PE, lambda: nt.matmul(S[:, :].rearrange("p (h t) -> p h t", h=4), lhsT=Kc[:, kc * 128:(kc + 1) * 128],
                                                rhs=Qg[:, :, b * 128:(b + 1) * 128], start=True, stop=True),
                          [Kc.b, Qg.b], [S.b], self_sync=False)
                    S_t[i] = S

                def emit_pv(i):
                    b, kc, mk, first, lastk = flat[i]
                    S = S_t.pop(i)
                    if first:
                        acc[b] = (Oacc.next(), Dacc.next())
                    Oa_, Da_ = acc[b]
                    P = p_r.next()
                    fw.op(A, lambda: na.activation(out=P[:], in_=S[:], func=AF.Exp, scale=0.125), [S.b], [P.b])
                    if mk is not None:
                        for hh in range(4):
                            fw.op(V, lambda: nv.tensor_tensor(out=P[:, hh * 128:(hh + 1) * 128], in0=P[:, hh * 128:(hh + 1) * 128],
                                                              in1=tri_b[:, mk, :], op=ALU.mult), [P.b, tri_b.b], [P.b])
                    fw.op(PE, lambda: nt.matmul(Da_[0:64, :], lhsT=ones_bf[:, 0:64], rhs=P[:], start=first, stop=lastk),
                          [P.b, ones_bf.b], [Da_.b], self_sync=False)
                    fw.op(PE, lambda: nt.matmul(Oa_[0:64, :], lhsT=Vc[:, kc, :], rhs=P[:], start=first, stop=lastk),
                          [P.b, Vc.b], [Oa_.b], self_sync=False)
                    if lastk:
                        del acc[b]
                        dt_ = f_r.next(); ob = ob_r.next()
                        fw.op(V, lambda: nv.tensor_tensor(out=dt_[:], in0=Da_[0:64, :], in1=esr[:, hk, :], op=ALU.add), [Da_.b, esr.b], [dt_.b])
                        fw.op(A, lambda: na.activation(out=dt_[:], in_=dt_[:], func=AF.Ln), [dt_.b], [dt_.b])
                        fw.op(A, lambda: na.activation(out=dt_[:], in_=dt_[:], func=AF.Exp, scale=-1.0), [dt_.b], [dt_.b])
                        fw.op(V, lambda: nv.tensor_tensor(out=ob[:], in0=Oa_[0:64, :], in1=dt_[:], op=ALU.mult), [Oa_.b, dt_.b], [ob.b])
                        fw.dma(SP, Oc.ap[hk * 256:(hk + 1) * 256, b * 128:(b + 1) * 128].rearrange("(h d) t -> d h t", d=64),
                               ob[:, :].rearrange("d (h t) -> d h t", h=4), [ob.b], Oc.B(hk * 256, (hk + 1) * 256, b * 128, (b + 1) * 128))
                SK = 3
                for i in range(len(flat) + SK):
                    if i < len(flat):
                        emit_qk(i)
                    if i >= SK:
                        emit_pv(i - SK)

    def mixB_phase(l, with_ctx):
        with fw.phase():
            SBs = [[fw.sbuf("SB%d%d" % (d, pr_), [128, 34, 128], BF16) for pr_ in range(2)] for d in range(2)]
            S32 = [[fw.sbuf("S32%d%d" % (d, pr_), [128, 128], F32) for pr_ in range(2)] for d in range(2)]
            gw = fw.sbuf("gw", [64, 512], BF16)
            lra = fw.sbuf("lra", [64, NT], BF16)
            fw.dma(G, gw[:], gw2_in[l], [], [gw.b])
            fw.op(V, lambda: nv.memset(lra[32:64, :], 1.0), [], [lra.b])
            fw.dma(SP, lra[0:32, :], Blr.ap[:, :], Blr.B(0, 32, 0, NT), [lra.b])
            order = {0: [32, 33] + list(range(32)), 1: [33, 32] + list(range(31, -1, -1))}
            for d in range(2):
                for pr_ in range(2):
                    fw.op(V, lambda: nv.memset(S32[d][pr_][:], 0.0), [], [S32[d][pr_].b])
            def pipeline(gens):
                gens = list(gens)
                active = []
                gi = 0
                while gi < len(gens) or active:
                    if gi < len(gens):
                        active.append(gens[gi]); gi += 1
                    nxt = []
                    for g_ in active:
                        try:
                            next(g_); nxt.append(g_)
                        except StopIteration:
                            pass
                    active = nxt

            kt_r = fw.ring("ktb", [128, 256], BF16, 3)
            vt_r = fw.ring("vtb", [128, 512], BF16, 5)
            sp1_r = fw.ring("sp1", [128, 256], F32, 4)
            e3_r = fw.ring("e3b", [128, 256], F32, 3)
            kd_r = fw.ring("kdb", [128, 256], BF16, 4)
            dec_r = fw.ring("decb", [128, 4], F32, 4)

            def p1_gen(i, d):
                c = order[d][i]
                z = PS.next()
                fw.op(PE, lambda: nt.matmul(z[:, 0:256], lhsT=lra[:, c * 128:(c + 1) * 128], rhs=gw[:, d * 256:(d + 1) * 256],
                                            start=True, stop=True), [lra.b, gw.b], [z.b], self_sync=False)
                sp = sp1_r.next()
                fw.op(A, lambda: na.activation(out=sp[:], in_=z[:, 0:256], func=AF.Exp, scale=-1.0), [z.b], [sp.b])
                fw.op(A, lambda: na.activation(out=sp[:], in_=sp[:], func=AF.Ln, bias=1.0), [sp.b], [sp.b])
                yield
                kt = kt_r.next(); vt = vt_r.next()
                fw.dma(SP, kt[:], BkT.ap[c * 128:(c + 1) * 128, :], BkT.B(c * 128, (c + 1) * 128, 0, 256), [kt.b])
                fw.dma(SP, vt[:], Bv.ap[c * 128:(c + 1) * 128, :], Bv.B(c * 128, (c + 1) * 128, 0, 512), [vt.b])
                rem = PS.next()
                fw.op(PE, lambda: nt.matmul(rem[:, 0:256], lhsT=tri_f[:, 2 + d, :], rhs=sp[:], start=True, stop=True),
                      [tri_f.b, sp.b], [rem.b], self_sync=False)
                for pr_ in range(2):
                    fw.op(PE, lambda: nt.matmul(rem[:, 256 + 2 * pr_:258 + 2 * pr_], lhsT=sp[:, pr_ * 128:(pr_ + 1) * 128], rhs=ones_f[:, 0:2],
                                                start=True, stop=True), [sp.b, ones_f.b], [rem.b], self_sync=False)
                e3 = e3_r.next()
                fw.op(A, lambda: na.activation(out=e3[:], in_=rem[:, 0:256], func=AF.Exp, scale=-1.0 / 16), [rem.b], [e3.b])
                dec = dec_r.next()
                fw.op(A, lambda: na.activation(out=dec[:], in_=rem[:, 256:260], func=AF.Exp, scale=-1.0 / 16), [rem.b], [dec.b])
                kd = kd_r.next()
                fw.op(V, lambda: nv.tensor_tensor(out=kd[:], in0=kt[:], in1=e3[:], op=ALU.mult), [kt.b, e3.b], [kd.b])
                yield
                for pr_ in range(2):
                    kv = PS.next()
                    fw.op(PE, lambda: nt.matmul(kv[:, 0:256], lhsT=kd[:, pr_ * 128:(pr_ + 1) * 128], rhs=vt[:, pr_ * 256:(pr_ + 1) * 256],
                                                start=True, stop=True), [kd.b, vt.b], [kv.b], self_sync=False)
                    st = S32[d][pr_]
                    fw.op(G, lambda: ng.tensor_copy(out=SBs[d][pr_][:, c, :], in_=st[:]), [st.b], [SBs[d][pr_].b])
                    for hh in range(2):
                        fw.op(V, lambda: nv.scalar_tensor_tensor(
                            out=st[hh * 64:(hh + 1) * 64, :], in0=st[hh * 64:(hh + 1) * 64, :],
                            scalar=dec[hh * 64:(hh + 1) * 64, 2 * pr_:2 * pr_ + 1],
                            in1=kv[hh * 64:(hh + 1) * 64, hh * 128:(hh + 1) * 128], op0=ALU.mult, op1=ALU.add),
                            [st.b, dec.b, kv.b], [st.b])

            if "x" not in mixers:
                pipeline(p1_gen(i, d) for i in range(34) for d in range(2))

            sp_r = fw.ring("spb", [128, 512], F32, 3)
            e_r = fw.ring("eb", [128, 512], F32, 4)
            q_r = fw.ring("qtb", [128, 2, 128], BF16, 3)
            k_r = fw.ring("k2b", [128, 2, 128], BF16, 3)
            qe_r = fw.ring("qeb", [128, 2, 4, 128], BF16, 5)
            ke_r = fw.ring("keb", [128, 2, 4, 128], BF16, 4)
            for rg in (qe_r, ke_r):
                for tq in rg.items:
                    fw.op(V, lambda: nv.memset(tq[:], 0.0), [], [tq.b])
            at_r = fw.ring("atb", [128, 4, 128], BF16, 6)
            osb_r = fw.ring("osb", [128, 512], F32, 3)
            osq_r = fw.ring("osq", [128, 512], BF16, 3)
            rs_r = fw.ring("rsb", [128, 512], F32, 2)
            tt_r = fw.ring("ttb", [128, 512], F32, 2)
            rt_r = fw.ring("rtb", [128, 4, 128], BF16, 2)
            ob_r = fw.ring("obb", [128, 4, 128], BF16, 2)
            vt2_r = fw.ring("vt2", [128, 512], BF16, 3)

            def p2_gen(c):
                cs = slice(c * 128, (c + 1) * 128)
                z = PS.next()
                fw.op(PE, lambda: nt.matmul(z[:], lhsT=lra[:, cs], rhs=gw[:, :], start=True, stop=True), [lra.b, gw.b], [z.b], self_sync=False)
                sp = sp_r.next()
                fw.op(A, lambda: na.activation(out=sp[:], in_=z[:], func=AF.Exp, scale=-1.0), [z.b], [sp.b])
                fw.op(A, lambda: na.activation(out=sp[:], in_=sp[:], func=AF.Ln, bias=1.0), [sp.b], [sp.b])
                qT = q_r.next(); kT = k_r.next()
                fw.dma(SP, qT[:], Bq.ap[:, cs].rearrange("(r p) t -> p r t", p=128), Bq.B(0, 256, c * 128, (c + 1) * 128), [qT.b])
                fw.dma(SP, kT[:], Bk.ap[:, cs].rearrange("(r p) t -> p r t", p=128), Bk.B(0, 256, c * 128, (c + 1) * 128), [kT.b])
                yield
                cm = PS.next()
                for d in range(2):
                    for pr_ in range(2):
                        q = d * 2 + pr_
                        fw.op(PE, lambda: nt.matmul(cm[:, q * 128:(q + 1) * 128], lhsT=sp[:, d * 256 + pr_ * 128:d * 256 + (pr_ + 1) * 128],
                                                    rhs=tri_f[:, d, :], start=True, stop=True), [sp.b, tri_f.b], [cm.b], self_sync=False)
                E1 = e_r.next(); E2 = e_r.next()
                fw.op(A, lambda: na.activation(out=E1[:], in_=cm[:], func=AF.Exp, scale=-1.0 / 16), [cm.b], [E1.b])
                fw.op(A, lambda: na.activation(out=E2[:], in_=cm[:], func=AF.Exp, scale=1.0 / 16), [cm.b], [E2.b])
                qe = qe_r.next(); ke = ke_r.next()
                for d in range(2):
                    for hh in range(2):
                        pp = slice(hh * 64, (hh + 1) * 64)
                        fw.op(V, lambda: nv.scalar_tensor_tensor(out=qe[pp, hh, 2 * d:2 * d + 2, :], in0=qT[pp], scalar=0.125,
                                                                 in1=E1[pp, d * 256:(d + 1) * 256].rearrange("p (r t) -> p r t", r=2),
                                                                 op0=ALU.mult, op1=ALU.mult), [qT.b, E1.b], [qe.b])
                        fw.op(V, lambda: nv.tensor_tensor(out=ke[pp, hh, 2 * d:2 * d + 2, :], in0=kT[pp],
                                                          in1=E2[pp, d * 256:(d + 1) * 256].rearrange("p (r t) -> p r t", r=2), op=ALU.mult),
                              [kT.b, E2.b], [ke.b])
                yield
                atts = []
                for d in range(2):
                    at = PS.next()
                    for hd in range(4):
                        pr_, hh = divmod(hd, 2)
                        fw.op(PE, lambda: nt.matmul(at[:, hd * 128:(hd + 1) * 128], lhsT=ke[:, hh, 2 * d + pr_, :],
                                                    rhs=qe[:, hh, 2 * d + pr_, :], start=True, stop=True),
                              [ke.b, qe.b], [at.b], self_sync=False)
                    att = at_r.next()
                    for hd in range(4):
                        fw.op(V, lambda: nv.tensor_tensor(out=att[:, hd, :], in0=at[:, hd * 128:(hd + 1) * 128],
                                                          in1=tri_f[:, d, :], op=ALU.mult),
                              [at.b, tri_f.b], [att.b])
                    atts.append(att)
                vt = vt2_r.next()
                fw.dma(SP, vt[:], Bv.ap[cs, :], Bv.B(c * 128, (c + 1) * 128, 0, 512), [vt.b])
                yield
                po = PS.next()
                for hd in range(4):
                    pr_, hh = divmod(hd, 2)
                    hs = slice(hd * 128, (hd + 1) * 128)
                    seq = [(vt[:, hs], atts[0][:, hd, :], [vt.b, atts[0].b]),
                           (SBs[0][pr_][:, c, :], qe[:, hh, pr_, :], [SBs[0][pr_].b, qe.b]),
                           (vt[:, hs], atts[1][:, hd, :], [vt.b, atts[1].b]),
                           (SBs[1][pr_][:, c, :], qe[:, hh, 2 + pr_, :], [SBs[1][pr_].b, qe.b])]
                    for qi, (lh, rh, rd) in enumerate(seq):
                        fw.op(PE, lambda: nt.matmul(po[:, hs], lhsT=lh, rhs=rh, start=(qi == 0), stop=(qi == 3)), rd, [po.b], self_sync=False)
                osb = osb_r.next(); osq = osq_r.next()
                fw.op(A, lambda: na.activation(out=osb[:], in_=po[:], func=AF.Copy), [po.b], [osb.b])
                fw.op(G, lambda: ng.tensor_tensor(out=osq[:], in0=osb[:], in1=osb[:], op=ALU.mult), [osb.b], [osq.b])
                yield
                rs = rs_r.next(); tt = tt_r.next()
                ss = PS.next()
                fw.op(PE, lambda: nt.matmul(ss[:], lhsT=ones_bf[:], rhs=osq[:], start=True, stop=True), [osq.b, ones_bf.b], [ss.b], self_sync=False)
                fw.op(A, lambda: na.activation(out=rs[:], in_=ss[:], func=AF.Ln, scale=1.0 / 128, bias=eps_t[:, 0:1]), [ss.b, eps_t.b], [rs.b])
                fw.op(A, lambda: na.activation(out=rs[:], in_=rs[:], func=AF.Exp, scale=-0.5), [rs.b], [rs.b])
                fw.op(V, lambda: nv.scalar_tensor_tensor(out=tt[:], in0=osb[:], scalar=gn[:, l:l + 1], in1=rs[:], op0=ALU.mult, op1=ALU.mult),
                      [osb.b, rs.b, gn.b], [tt.b])
                rT = rt_r.next()
                fw.dma(SP, rT[:], Br.ap[:, cs].rearrange("(h p) t -> p h t", p=128), Br.B(0, 512, c * 128, (c + 1) * 128), [rT.b])
                ob = ob_r.next()
                fw.op(G, lambda: ng.tensor_tensor(out=ob[:], in0=tt[:, :].rearrange("p (h t) -> p h t", h=4), in1=rT[:], op=ALU.mult),
                      [tt.b, rT.b], [ob.b])
                fw.dma(SP, Ob.ap[:, cs].rearrange("(h p) t -> p h t", p=128), ob[:], [ob.b], Ob.B(0, 512, c * 128, (c + 1) * 128))

            if "y" not in mixers:
                pipeline(p2_gen(c) for c in (range(34) if with_ctx else range(32)))

    def merge_phase(l, with_ctx):
        Wm = WB[l]["mix"]
        with fw.phase():
            h2 = fw.sbuf("h2m", [128, 8, 1024], BF16)
            oa = fw.sbuf("oam", [128, 4, 1024], BF16)
            obm = fw.sbuf("obm", [128, 4, 1024], BF16)
            ocm = fw.sbuf("ocm", [64, 8, 1024], BF16)
            y = fw.sbuf("ym", [128, 8, 1024], BF16)
            Wa = fw.sbuf("Wa", [128, 4, D], BF16)
            Wb = fw.sbuf("Wb", [128, 4, D], BF16)
            Wc = fw.sbuf("Wc", [64, 8, D], BF16)
            Wo = fw.sbuf("Wo", [128, 8, D], BF16)
            wg_r = fw.ring("wg", [128, 8, 3, 128], BF16, 2)
            sg_r = fw.ring("sgm", [128, 512], F32, 3)
            y_r = fw.ring("yr", [128, 512], F32, 4)
            xr_r = fw.ring("xr", [128, 512], F32, 3)
            xo_r = fw.ring("xo", [128, 512], F32, 3)
            ld_w(SP, Wa[:], WB[l]["br"][0], 0, 4, 0, D, Wa)
            ld_w(SP, Wb[:], WB[l]["br"][1], 0, 4, 0, D, Wb)
            apc, bc = WB[l]["br"][2]
            fw.dma(SP, Wc[:], apc.rearrange("(h d) c -> d h c", d=64), [bc], [Wc.b])
            ld_w(SP, Wo[:], WB[l]["out"], 0, 8, 0, D, Wo)
            sts = [(t0, n) for (t0, n) in STS if not (t0 >= SEQ and not with_ctx)]

            def ld_inputs(t0, n):
                fw.dma(SP, h2[:, :, 0:n], h2s.ap[:, t0:t0 + n].rearrange("(k p) n -> p k n", p=128), h2s.B(0, D, t0, t0 + n), [h2.b])
                fw.dma(SP, oa[:, :, 0:n], Oa.ap[:, t0:t0 + n].rearrange("(k p) n -> p k n", p=128), Oa.B(0, 512, t0, t0 + n), [oa.b])
                fw.dma(SP, obm[:, :, 0:n], Ob.ap[:, t0:t0 + n].rearrange("(k p) n -> p k n", p=128), Ob.B(0, 512, t0, t0 + n), [obm.b])
                fw.dma(SP, ocm[:, :, 0:n], Oc.ap[:, t0:t0 + n].rearrange("(h d) n -> d h n", d=64), Oc.B(0, 512, t0, t0 + n), [ocm.b])

            def ld_wg(oc):
                wg = wg_r.next()
                for br in range(3):
                    ld_w(SP, wg[:, :, br, :], Wm, 0, 8, O_G + br * D + oc * 128, 128, wg)
                return wg

            ld_inputs(*sts[0])
            nxt_wg = ld_wg(0)
            for si, (t0, n) in enumerate(sts):
                isctx = t0 >= SEQ
                tls = tiles_of(0, n)
                for oc in range(8):
                    wg = nxt_wg
                    if oc + 1 < 8:
                        nxt_wg = ld_wg(oc + 1)
                    elif si + 1 < len(sts):
                        nxt_wg = ld_wg(0)
                    for (o, m) in tls:
                        pgs = []
                        for br in range(3):
                            pg = PS.next()
                            for kc in range(8):
                                fw.op(PE, lambda: nt.matmul(pg[:, 0:m], lhsT=wg[:, kc, br, :], rhs=h2[:, kc, o:o + m],
                                                            start=(kc == 0), stop=(kc == 7)), [wg.b, h2.b], [pg.b], self_sync=False)
                            pgs.append(pg)
                        ys = []
                        for br in range(3):
                            pb = PS.next()
                            if br < 2:
                                Wt, src = (Wa, oa) if br == 0 else (Wb, obm)
                                for kc in range(4):
                                    fw.op(PE, lambda: nt.matmul(pb[:, 0:m], lhsT=Wt[:, kc, oc * 128:(oc + 1) * 128], rhs=src[:, kc, o:o + m],
                                                                start=(kc == 0), stop=(kc == 3)), [Wt.b, src.b], [pb.b], self_sync=False)
                            else:
                                for hd in range(8):
                                    fw.op(PE, lambda: nt.matmul(pb[:, 0:m], lhsT=Wc[:, hd, oc * 128:(oc + 1) * 128], rhs=ocm[:, hd, o:o + m],
                                                                start=(hd == 0), stop=(hd == 7)), [Wc.b, ocm.b], [pb.b], self_sync=False)
                            sg_ = sg_r.next()
                            fw.op(A, lambda: na.activation(out=sg_[:, 0:m], in_=pgs[br][:, 0:m], func=AF.Sigmoid), [pgs[br].b], [sg_.b])
                            yt = y_r.next()
                            fw.op(V, lambda: nv.tensor_tensor(out=yt[:, 0:m], in0=sg_[:, 0:m], in1=pb[:, 0:m], op=ALU.mult), [sg_.b, pb.b], [yt.b])
                            ys.append(yt)
                        y12 = y_r.next()
                        fw.op(G, lambda: ng.tensor_tensor(out=y12[:, 0:m], in0=ys[0][:, 0:m], in1=ys[1][:, 0:m], op=ALU.add), [ys[0].b, ys[1].b], [y12.b])
                        fw.op(G, lambda: ng.tensor_tensor(out=y[:, oc, o:o + m], in0=y12[:, 0:m], in1=ys[2][:, 0:m], op=ALU.add), [y12.b, ys[2].b], [y.b])
                if si + 1 < len(sts):
                    ld_inputs(*sts[si + 1])
                for oc in range(8):
                    for (o, m) in tls:
                        pt = PS.next()
                        for kc in range(8):
                            fw.op(PE, lambda: nt.matmul(pt[:, 0:m], lhsT=Wo[:, kc, oc * 128:(oc + 1) * 128], rhs=y[:, kc, o:o + m],
                                                        start=(kc == 0), stop=(kc == 7)), [Wo.b, y.b], [pt.b], self_sync=False)
                        residual_out(l, 1, pt, oc, t0 + o, m, isctx, xr_r, xo_r)

    def final_phase():
        with fw.phase():
            rings = (fw.ring("xt", [128, 8, 256], F32, 2), fw.ring("sq", [128, 256], BF16, 3),
                     fw.ring("rs", [128, 256], F32, 2), fw.ring("tm", [128, 256], F32, 3))
            fo_r = fw.ring("fo", [128, 8, 256], F32, 2)
            for t0 in range(0, SEQ, 256):
                xt, rs = norm_mod(0, 0, xs, t0, 256, False, None, 0, rings)
                fo = fo_r.next()
                for kc in range(8):
                    fw.op(V, lambda: nv.scalar_tensor_tensor(out=fo[:, kc, :], in0=xt[:, kc, :], scalar=fg[:, kc:kc + 1], in1=rs[:, 0:256],
                                                             op0=ALU.mult, op1=ALU.mult), [xt.b, rs.b, fg.b], [fo.b])
                fw.dma(SP, out_dt.ap[:, t0:t0 + 256].rearrange("(k p) n -> p k n", p=128), fo[:], [fo.b], out_dt.B(0, D, t0, t0 + 256))

    for l in range(n_layers):
        last = (l == L_ALL - 1)
        with_ctx = not last
        ffn_phase(l, 0)
        if stop_after == ("ffn1", l):
            break
        mixin_phase(l)
        if stop_after == ("mixin", l):
            break
        def _casts(after=(), l=l):
            if l + 1 < n_layers:
                convert_layer(l + 1, 0, after)
                convert_layer(l + 1, 1, after)
        if "A" in mixers:
            mixA_phase(l, with_ctx, _casts)
        else:
            _casts()
        if "C" in mixers:
            mixC_phase(l, with_ctx)
        if "B" in mixers:
            mixB_phase(l, with_ctx)
        if stop_after == ("mixers", l):
            break
        merge_phase(l, with_ctx)
        if stop_after == ("mix", l):
            break
        ffn_phase(l, 1, skip_ctx=last)
        if stop_after == ("ffn2", l):
            break
    else:
        if stop_after is None:
            final_phase()

    if isinstance(dump, (list, tuple)):
        dts = dict(xs=xs, h2s=h2s, Aq=Aq, Ak=Ak, Av=Av, Bq=Bq, Bk=Bk, BkT=BkT, Bv=Bv, Blr=Blr, Br=Br, Cq=Cq, Ck=Ck, Cv=Cv, Oa=Oa, Ob=Ob, Oc=Oc)
        for nm in dump:
            src = dts[nm]
            shp = list(src.ap.shape)
            dd = nc.dram_tensor("dbg_" + nm, shp, F32, kind="ExternalOutput").ap()
            fw.dma(G, dd, src.ap, src.B(0, shp[0], 0, shp[1]), [Buf("dbg")])
    if dump == "mod":
        ob = out_dt.B(0, D, 0, SEQ)
        fw.dma(SP, out_dt.ap[0:128, 0:144], modT[:, 0].rearrange("p i c w -> p (i c w)"), [modT.b], ob)
        fw.dma(SP, out_dt.ap[0:128, 144:192], gsT[:, 0].rearrange("p i c w -> p (i c w)"), [gsT.b], ob)
        fw.dma(SP, out_dt.ap[0:128, 192:240], cfT[:, 0].rearrange("p i c w -> p (i c w)"), [cfT.b], ob)
    if dump == "xs":
        with fw.phase():
            cp_r = fw.ring("cp", [128, 8, 256], F32, 2)
            for t0 in range(0, SEQ, 256):
                cp = cp_r.next()
                fw.dma(SP, cp[:], xs.ap[:, t0:t0 + 256].rearrange("(k p) n -> p k n", p=128), xs.B(0, D, t0, t0 + 256), [cp.b])
                fw.dma(SP, out_dt.ap[:, t0:t0 + 256].rearrange("(k p) n -> p k n", p=128), cp[:], [cp.b], out_dt.B(0, D, t0, t0 + 256))
    fw.barrier()
    fw.close()
    return nc


def _rope_tables_np():
    rows = SEQ // 64
    row = np.repeat(np.arange(rows, dtype=np.float32), 64)
    col = np.tile(np.arange(64, dtype=np.float32), rows)
    nf = 16
    freqs = np.power(np.float32(10000.0), -np.arange(nf, dtype=np.float32) / nf).astype(np.float32)
    ar = row[:, None] * freqs
    ac = col[:, None] * freqs
    ang = np.concatenate([ar, ar, ac, ac], axis=-1).astype(np.float32)
    cos = np.cos(ang).astype(np.float32)
    sin = np.sin(ang).astype(np.float32)
    sign = np.ones(64, np.float32)
    for a in range(2):
        sign[a * 32:a * 32 + 16] = -1.0
    sinS = sin * sign[None, :]
    C = np.ones((64, NT), np.float32)
    S = np.zeros((64, NT), np.float32)
    C[:, :SEQ] = cos.T
    S[:, :SEQ] = sinS.T
    return np.ascontiguousarray(np.concatenate([C, C], 0)), np.ascontiguousarray(np.concatenate([S, S], 0))


def _perm64():
    p = np.zeros(64, np.int64)
    for a in range(2):
        for i in range(16):
            p[a * 32 + i] = a * 32 + 16 + i
            p[a * 32 + 16 + i] = a * 32 + i
    return p


def _perm_cols():
    p64 = _perm64()
    cols = []
    for (base, nh) in ((O_AQ, 8), (O_AK, 8), (O_CQ, 8), (O_CK, 2)):
        for hh in range(nh):
            cols.extend((base + hh * 64 + p64).tolist())
    return np.asarray(cols, np.int64)


def prep_shared(inp):
    f = np.float32
    sh = {}
    sh["w_ada"] = np.ascontiguousarray(inp["w_ada"], dtype=f)
    sh["b_adaT"] = np.ascontiguousarray(inp["b_ada"].reshape(L_ALL, 72, 128).transpose(2, 0, 1), dtype=f)
    sh["normgT"] = np.ascontiguousarray(inp["norm_g"].reshape(L_ALL, 3, 8, 128).transpose(3, 0, 1, 2), dtype=f)
    sh["fgT"] = np.ascontiguousarray(inp["final_g"].reshape(8, 128).T, dtype=f)
    for k in ("w_ffn1_in", "w_ffn1_out", "w_ffn2_in", "w_ffn2_out", "w_mix_in", "w_br_a", "w_br_b", "w_br_c", "w_mix_out"):
        sh[k] = np.ascontiguousarray(inp[k], dtype=f)
    sh["w_perm"] = np.ascontiguousarray(inp["w_mix_in"][:, :, _perm_cols()], dtype=f)
    C, S = _rope_tables_np()
    sh["ropeC"] = C
    sh["ropeS"] = S
    sh["dlam"] = np.ascontiguousarray(np.broadcast_to(inp["diff_lambda"][None], (128, L_ALL, 4, 64)), dtype=f)
    sh["sublnT"] = np.ascontiguousarray(inp["diff_subln"].T, dtype=f)
    sh["gnT"] = np.ascontiguousarray(inp["gla_norm"].T, dtype=f)
    gw2 = np.zeros((L_ALL, 64, 512), f)
    gw2[:, 0:16, 0:256] = inp["gla_gate_w"][:, 0]
    gw2[:, 16:32, 256:512] = inp["gla_gate_w"][:, 1]
    gw2[:, 32, 0:256] = inp["gla_gate_b"][:, 0]
    gw2[:, 32, 256:512] = inp["gla_gate_b"][:, 1]
    sh["gw2"] = gw2
    sh["sinkB"] = np.ascontiguousarray(np.broadcast_to(inp["swa_sink"][None], (128, L_ALL, 8)), dtype=f)
    s = np.arange(128)[:, None]
    t = np.arange(128)[None, :]
    tri = np.stack([(s <= t), (s >= t), (s > t), (s < t)], axis=1).astype(f)
    sh["tri"] = np.ascontiguousarray(tri)
    return sh


def prep_core(inp, b):
    f = np.float32
    m = {}
    m["xc"] = np.ascontiguousarray(np.concatenate([inp["x"][b].T, inp["ctx"][b].T], axis=1), dtype=f)
    sc = np.stack([inp["c"][b].reshape(8, 128).T, inp["c_ctx"].reshape(8, 128).T], axis=-1)
    m["scT"] = np.ascontiguousarray(sc, dtype=f)
    return m


_NC_CACHE = {}


def kernel(**inputs):
    inp = {k: np.asarray(v) for k, v in inputs.items()}
    if "full" not in _NC_CACHE:
        _NC_CACHE["full"] = build()
    nc = _NC_CACHE["full"]
    sh = prep_shared(inp)
    in_maps = []
    for b in range(8):
        m = dict(sh)
        m.update(prep_core(inp, b))
        in_maps.append(m)
    res = run_bass_kernel_spmd(nc, in_maps, core_ids=list(range(8)))
    out = np.stack([np.ascontiguousarray(res.results[b]["out"].T) for b in range(8)], axis=0)
    return out.astype(np.float32)
```
